# Optimizing a Trainium2 kernel written in Bass

```python
import math
import jax, jax.numpy as jnp
from jax import lax
import numpy as np

D_MODEL = 1024
BATCH = 1
SEQ = 16384
DEPTH = 2
DEC_BATCH = 16
DEC_SEQ = 2048
PAST_LEN = 128

CONV_W = 512
CONV_K = 31
N_HEADS = 8
HEAD_DIM = 64
ATTN_W = N_HEADS * 2 * HEAD_DIM
Q_BLOCK = 128
ROPE_THETA = 10000.0
SGU_W = 512
SGU_GROUPS = 4
SGU_GROUP_W = SGU_W // SGU_GROUPS
CHUNK = 128
N_BRANCH = 3
RMS_EPS = 1e-6

_COLS = (
    2 * CONV_W,
    CONV_W,
    ATTN_W,
    ATTN_W,
    ATTN_W,
    ATTN_W,
    SGU_W,
    SGU_W,
    SGU_W,
    N_BRANCH * D_MODEL,
)
IN_COLS = sum(_COLS)
SPLITS = tuple(int(s) for s in np.cumsum(_COLS)[:-1])

kernel_name = "hybrid_conv_diffattn_gmlp_encoder"


def lambda_init_fn(layer_idx):
    return 0.8 - 0.6 * math.exp(-0.3 * layer_idx)


def rmsnorm(x, g):
    xf = x.astype(jnp.float32)
    y = xf * lax.rsqrt(jnp.mean(xf * xf, axis=-1, keepdims=True) + RMS_EPS)
    return (y * g.astype(jnp.float32)).astype(x.dtype)


def rotary(x, pos):
    half = HEAD_DIM // 2
    inv = ROPE_THETA ** (-jnp.arange(half, dtype=jnp.float32) / half)
    ang = pos.astype(jnp.float32)[:, None] * inv[None, :]
    cos = jnp.cos(ang)[None, :, None, None, :]
    sin = jnp.sin(ang)[None, :, None, None, :]
    xf = x.astype(jnp.float32)
    x1, x2 = xf[..., :half], xf[..., half:]
    out = jnp.concatenate([x1 * cos - x2 * sin, x2 * cos + x1 * sin], axis=-1)
    return out.astype(x.dtype)


def depthwise_conv(x, w, b):
    c = x.shape[-1]
    y = lax.conv_general_dilated(
        x, w[:, None, :].astype(x.dtype), window_strides=(1,),
        padding=[(CONV_K // 2, CONV_K // 2)],
        dimension_numbers=("NWC", "WIO", "NWC"), feature_group_count=c)
    return y + b.astype(x.dtype)


def diff_attention(q, k, v, lam):
    b, s = q.shape[0], q.shape[1]
    nblk = s // Q_BLOCK
    scale = 1.0 / math.sqrt(HEAD_DIM)
    qb = q.reshape(b, nblk, Q_BLOCK, N_HEADS, 2, HEAD_DIM).transpose(1, 0, 2, 3, 4, 5)
    kf = k.astype(jnp.float32)
    vf = v.astype(jnp.float32)

    def block(qblk):
        sc = jnp.einsum("bqhcd,bkhcd->bchqk", qblk.astype(jnp.float32), kf) * scale
        p = jax.nn.softmax(sc, axis=-1)
        a = p[:, 0] - lam * p[:, 1]
        return jnp.einsum("bhqk,bkhd->bqhd", a, vf)

    o = lax.map(block, qb)
    return o.transpose(1, 0, 2, 3, 4).reshape(b, s, N_HEADS, 2 * HEAD_DIM)


def mixer_layer(x, l, pos, norm_g, w_in, conv_w, conv_b, conv_norm_g, w_proj_a,
                lam_q1, lam_k1, lam_q2, lam_k2, subln_g, w_proj_b,
                sgu_norm_g, sgu_w, sgu_b, w_proj_c, w_out):
    b, s, _ = x.shape
    h = rmsnorm(x, norm_g)
    z = h @ w_in
    a_in, a_gate, q, k, v, b_gate, c_u, c_v, c_gate, merge = jnp.split(z, SPLITS, axis=-1)

    a = a_in[..., :CONV_W] * jax.nn.sigmoid(a_in[..., CONV_W:])
    a = depthwise_conv(a, conv_w, conv_b)
    a = jax.nn.silu(rmsnorm(a, conv_norm_g))
    ya = (a * jax.nn.silu(a_gate)) @ w_proj_a

    lam_init = lambda_init_fn(l)
    q = rotary(q.reshape(b, s, N_HEADS, 2, HEAD_DIM), pos)
    k = rotary(k.reshape(b, s, N_HEADS, 2, HEAD_DIM), pos)
    v = v.reshape(b, s, N_HEADS, 2 * HEAD_DIM)
    lam = (jnp.exp(jnp.sum(lam_q1.astype(jnp.float32) * lam_k1.astype(jnp.float32)))
           - jnp.exp(jnp.sum(lam_q2.astype(jnp.float32) * lam_k2.astype(jnp.float32)))
           + lam_init)
    o = diff_attention(q, k, v, lam)
    o = rmsnorm(o, subln_g) * (1.0 - lam_init)
    yb = (o.reshape(b, s, ATTN_W).astype(x.dtype) * jax.nn.silu(b_gate)) @ w_proj_b

    cv = rmsnorm(c_v, sgu_norm_g).reshape(b, s // CHUNK, CHUNK, SGU_GROUPS, SGU_GROUP_W)
    mixed = jnp.einsum("gpq,bnqgd->bnpgd", sgu_w, cv) + sgu_b.T[None, None, :, :, None]
    yc = (c_u * mixed.reshape(b, s, SGU_W) * jax.nn.silu(c_gate)) @ w_proj_c

    m = jax.nn.sigmoid(merge).reshape(b, s, N_BRANCH, D_MODEL)
    y = m[:, :, 0] * ya + m[:, :, 1] * yb + m[:, :, 2] * yc
    return x + y @ w_out


def trunk(x, norm_g, w_in, conv_w, conv_b, conv_norm_g, w_proj_a,
          lam_q1, lam_k1, lam_q2, lam_k2, subln_g, w_proj_b,
          sgu_norm_g, sgu_w, sgu_b, w_proj_c, w_out, final_g):
    pos = jnp.arange(x.shape[1], dtype=jnp.int32)
    for l in range(DEPTH):
        x = mixer_layer(x, l, pos, norm_g[l], w_in[l], conv_w[l], conv_b[l], conv_norm_g[l], w_proj_a[l],
                        lam_q1[l], lam_k1[l], lam_q2[l], lam_k2[l], subln_g[l], w_proj_b[l],
                        sgu_norm_g[l], sgu_w[l], sgu_b[l], w_proj_c[l], w_out[l])
    return rmsnorm(x, final_g)


def setup_inputs(seed: int = 0) -> dict:
    key = jax.random.key(seed)
    ks = jax.random.split(key, 22)
    f32 = jnp.float32

    def nrm(k, shape, scale):
        return jax.random.normal(k, shape, f32) * scale

    return {
        "x_prompt": nrm(ks[0], (BATCH, SEQ, D_MODEL), 1.0),
        "x_sample": nrm(ks[1], (DEC_BATCH, DEC_SEQ, D_MODEL), 1.0),
        "norm_g": 1.0 + nrm(ks[2], (DEPTH, D_MODEL), 0.02),
        "w_in": nrm(ks[3], (DEPTH, D_MODEL, IN_COLS), D_MODEL ** -0.5),
        "conv_w": nrm(ks[4], (DEPTH, CONV_K, CONV_W), CONV_K ** -0.5),
        "conv_b": nrm(ks[5], (DEPTH, CONV_W), 0.02),
        "conv_norm_g": 1.0 + nrm(ks[6], (DEPTH, CONV_W), 0.02),
        "w_proj_a": nrm(ks[7], (DEPTH, CONV_W, D_MODEL), CONV_W ** -0.5),
        "lam_q1": nrm(ks[8], (DEPTH, HEAD_DIM), 0.1),
        "lam_k1": nrm(ks[9], (DEPTH, HEAD_DIM), 0.1),
        "lam_q2": nrm(ks[10], (DEPTH, HEAD_DIM), 0.1),
        "lam_k2": nrm(ks[11], (DEPTH, HEAD_DIM), 0.1),
        "subln_g": 1.0 + nrm(ks[12], (DEPTH, 2 * HEAD_DIM), 0.02),
        "w_proj_b": nrm(ks[13], (DEPTH, ATTN_W, D_MODEL), ATTN_W ** -0.5),
        "sgu_norm_g": 1.0 + nrm(ks[14], (DEPTH, SGU_W), 0.02),
        "sgu_w": nrm(ks[15], (DEPTH, SGU_GROUPS, CHUNK, CHUNK), CHUNK ** -0.5),
        "sgu_b": 1.0 + nrm(ks[16], (DEPTH, SGU_GROUPS, CHUNK), 0.02),
        "w_proj_c": nrm(ks[17], (DEPTH, SGU_W, D_MODEL), SGU_W ** -0.5),
        "w_out": nrm(ks[18], (DEPTH, D_MODEL, D_MODEL), D_MODEL ** -0.5),
        "final_g": 1.0 + nrm(ks[19], (D_MODEL,), 0.02),
    }


def reference(x_prompt, x_sample, norm_g, w_in, conv_w, conv_b, conv_norm_g, w_proj_a,
              lam_q1, lam_k1, lam_q2, lam_k2, subln_g, w_proj_b,
              sgu_norm_g, sgu_w, sgu_b, w_proj_c, w_out, final_g):
    y_prompt = trunk(x_prompt, norm_g, w_in, conv_w, conv_b, conv_norm_g, w_proj_a,
                     lam_q1, lam_k1, lam_q2, lam_k2, subln_g, w_proj_b,
                     sgu_norm_g, sgu_w, sgu_b, w_proj_c, w_out, final_g)
    y_sample = trunk(x_sample, norm_g, w_in, conv_w, conv_b, conv_norm_g, w_proj_a,
                     lam_q1, lam_k1, lam_q2, lam_k2, subln_g, w_proj_b,
                     sgu_norm_g, sgu_w, sgu_b, w_proj_c, w_out, final_g)
    return (y_prompt, y_sample)
```

```python
import math
from contextlib import ExitStack

import numpy as np
import concourse.bass as bass
import concourse.mybir as mybir
from concourse.bass_utils import run_bass_kernel_spmd

F32 = mybir.dt.float32
BF16 = mybir.dt.bfloat16
I32 = mybir.dt.int32
AF = mybir.ActivationFunctionType
ALU = mybir.AluOpType
AX = mybir.AxisListType

NCORES = 8
D = 1024
KC = 8
SEG = 2048
NSEG = 3
T = SEG * NSEG
BLK = 512
NB = SEG // BLK
L = 2
INC = 10240
CONV_K = 31
HALO = 15
EPS = 1e-6
TWO_PI = float(2.0 * np.pi)

CH_AIN, CH_AGATE, CH_Q, CH_K, CH_V, CH_BG, CH_CU, CH_CV, CH_CG, CH_M = 0, 8, 12, 20, 28, 36, 44, 48, 52, 56

O_GN = 0
O_FG = O_GN + L * 8
O_CW = O_FG + 8
O_CB = O_CW + L * 4 * CONV_K
O_CG = O_CB + L * 4
O_SUBG = O_CG + L * 4
O_INVF = O_SUBG + L
O_SGN = O_INVF + 1
O_SEL = O_SGN + 1
O_LAMV = O_SEL + 16
O_SGNG = O_LAMV + L * 4 * 64
O_SGUB = O_SGNG + L * 512
NPRM = O_SGUB + L * 4 * 128

DEBUG_OUT = set()

CE = ("pe", "act", "dve", "pool")
ENGS = ("pe", "act", "dve", "pool", "sp")


class Buf:
    __slots__ = ("name", "w", "r")

    def __init__(self, name):
        self.name = name
        self.w = {}
        self.r = {}


def bufs(name, n):
    return [Buf(f"{name}{i}") for i in range(n)]


def flat(x):
    if isinstance(x, Buf):
        return [x]
    out = []
    for e in x:
        out.extend(flat(e))
    return out


class Prog:
    def __init__(self, nc, stack, n_dma_sems=28):
        self.nc = nc
        self.ops = {e: [] for e in ENGS}
        self.cnt = {e: 0 for e in CE}
        self.sem = {}
        for e in CE:
            self.sem[e] = stack.enter_context(nc.semaphore("s_" + e))
        self.dsems = []
        for i in range(n_dma_sems):
            nm = f"d{i}"
            self.sem[nm] = stack.enter_context(nc.semaphore(nm))
            self.dsems.append(nm)
        self.dcnt = {nm: 0 for nm in self.dsems}
        self.drr = 0
        self.known = {e: {} for e in ENGS}
        self.stack = stack
        self.ncc = 0

    def _need(self, waits, toks):
        for s, v in toks.items():
            if waits.get(s, 0) < v:
                waits[s] = v

    def _finish_waits(self, eng, waits):
        kn = self.known[eng]
        out = []
        for s, v in waits.items():
            if eng == "pe" and s == "pe":
                continue
            if kn.get(s, 0) < v:
                kn[s] = v
                out.append((s, v))
        return out

    def _deps(self, reads, writes):
        waits = {}
        for b in flat(reads):
            self._need(waits, b.w)
        for b in flat(writes):
            self._need(waits, b.w)
            self._need(waits, b.r)
        return waits

    def _commit(self, tok, reads, writes):
        s, v = tok
        for b in flat(reads):
            if b.r.get(s, 0) < v:
                b.r[s] = v
        for b in flat(writes):
            b.w = {s: v}
            b.r = {}

    def op(self, eng, fn, reads=(), writes=()):
        waits = self._deps(reads, writes)
        wl = self._finish_waits(eng, waits)
        self.cnt[eng] += 1
        tok = (eng, self.cnt[eng])
        self.ops[eng].append((wl, fn, (eng, 1)))
        self._commit(tok, reads, writes)
        return tok

    def dma(self, out, in_, reads=(), writes=(), q="sp"):
        waits = self._deps(reads, writes)
        nm = self.dsems[self.drr % len(self.dsems)]
        self.drr += 1
        if self.dcnt[nm] > 0:
            self._need(waits, {nm: self.dcnt[nm]})
        wl = self._finish_waits(q, waits)
        self.dcnt[nm] += 16
        tok = (nm, self.dcnt[nm])
        self.ops[q].append((wl, (lambda e, o=out, i=in_: e.dma_start(out=o, in_=i)), (nm, 16)))
        self._commit(tok, reads, writes)
        return tok

    def collective(self, in_ap, out_ap, groups, reads=(), writes=()):
        import os
        if os.environ.get("K_NOCC"):
            return None
        waits = self._deps(reads, writes)
        wl = self._finish_waits("pool", waits)
        nm = f"cc{self.ncc}"
        self.ncc += 1
        self.sem[nm] = self.stack.enter_context(self.nc.semaphore(nm))

        def fn(e, i=in_ap, o=out_ap, g=groups):
            return e.collective_compute("AllGather", ALU.bypass, replica_groups=g, ins=[i.opt()], outs=[o.opt()])
        self.ops["pool"].append((wl, fn, (nm, None)))
        tok = (nm, 1)
        self._commit(tok, reads, writes)
        return tok

    def barrier(self):
        allt = {e: self.cnt[e] for e in CE if self.cnt[e] > 0}
        for nm, v in self.dcnt.items():
            if v > 0:
                allt[nm] = v
        for e in ENGS:
            wl = self._finish_waits(e, dict(allt))
            if wl:
                self.ops[e].append((wl, None, None))

    def emit(self, block):
        sem = self.sem

        def run(e, lst):
            for wl, fn, inc in lst:
                for s, v in wl:
                    e.wait_ge(sem[s], v)
                if fn is None:
                    continue
                ins = fn(e)
                if inc[1] is None:
                    ins.then_inc(sem[inc[0]])
                else:
                    ins.then_inc(sem[inc[0]], inc[1])

        @block.tensor
        def _(e):
            run(e, self.ops["pe"])

        @block.scalar
        def _(e):
            run(e, self.ops["act"])

        @block.vector
        def _(e):
            run(e, self.ops["dve"])

        @block.gpsimd
        def _(e):
            run(e, self.ops["pool"])

        @block.sync
        def _(e):
            run(e, self.ops["sp"])


def lambda_init_fn(layer_idx):
    return 0.8 - 0.6 * math.exp(-0.3 * layer_idx)


def build_nc(stop_after=None):
    nc = bass.Bass("TRN2", target_bir_lowering=False)

    def dram_in(name, shape, dt=F32):
        return nc.dram_tensor(name, list(shape), dt, kind="ExternalInput").ap()

    def scratch(name, shape, dt):
        kind = "ExternalOutput" if name in DEBUG_OUT else "Internal"
        return nc.dram_tensor(name, list(shape), dt, kind=kind).ap()

    xT = dram_in("xT", [D, T])
    w_in = dram_in("w_in", [L, D, INC])
    w_pa = dram_in("w_pa", [L, 512, D])
    w_pb = dram_in("w_pb", [L, D, D])
    w_pc = dram_in("w_pc", [L, 512, D])
    w_out = dram_in("w_out", [L, D, D])
    prm_d = dram_in("prm", [128, NPRM])
    sgw_d = dram_in("sgw", [128, L * 4 * 128])
    pos_d = dram_in("pos", [128, 2 * SEG])
    yT = nc.dram_tensor("yT", [D, T], F32, kind="ExternalOutput").ap()

    x1T = scratch("x1T", [D, T], F32)
    x2T = scratch("x2T", [D, T], F32)
    hT_d = scratch("hT_d", [NSEG, D, SEG], BF16)
    qT_d = scratch("qT_d", [NSEG, D, SEG], BF16)
    kTs_d = scratch("kTs_d", [2, D, SEG], BF16)
    vs_d = scratch("vs_d", [2, 8, SEG, 128], BF16)
    gb_d = scratch("gb_d", [NSEG, D, SEG], BF16)
    af_d = scratch("af_d", [NSEG, 512, SEG], BF16)
    cf_d = scratch("cf_d", [NSEG, 512, SEG], BF16)
    og_d = scratch("og_d", [NSEG, D, SEG], BF16)
    ropeC = scratch("ropeC", [128, 2 * SEG], F32)
    ropeS = scratch("ropeS", [128, 2 * SEG], F32)
    kT_loc = [[scratch(f"kT_loc{l}_{h}", [128, SEG], BF16) for h in range(8)] for l in range(L)]
    kT_mid = [[scratch(f"kT_mid{l}_{h}", [4 * 128, SEG], BF16) for h in range(8)] for l in range(L)]
    kT_all = [[scratch(f"kT_all{l}_{h}", [8 * 128, SEG], BF16) for h in range(8)] for l in range(L)]
    v_loc = [[scratch(f"v_loc{l}_{h}", [SEG, 128], BF16) for h in range(8)] for l in range(L)]
    v_mid = [[scratch(f"v_mid{l}_{h}", [4 * SEG, 128], BF16) for h in range(8)] for l in range(L)]
    v_all = [[scratch(f"v_all{l}_{h}", [8 * SEG, 128], BF16) for h in range(8)] for l in range(L)]
    hl_loc = [scratch(f"hl_loc{l}", [512, 32], BF16) for l in range(L)]
    hl_mid = [scratch(f"hl_mid{l}", [4 * 512, 32], BF16) for l in range(L)]
    hl_all = [scratch(f"hl_all{l}", [8 * 512, 32], BF16) for l in range(L)]
    G4 = [[0, 1, 2, 3], [4, 5, 6, 7]]
    G2 = [[0, 4], [1, 5], [2, 6], [3, 7]]

    B_x = {"x0": [Buf(f"x0_{s}") for s in range(NSEG)], "x1": [Buf(f"x1_{s}") for s in range(NSEG)],
           "x2": [Buf(f"x2_{s}") for s in range(NSEG)]}
    B_hT = bufs("hTd", NSEG)
    B_qT = bufs("qTd", NSEG)
    B_kTs = bufs("kTsd", 2)
    B_vs = [bufs(f"vsd{i}_", 8) for i in range(2)]
    B_gb = bufs("gbd", NSEG)
    B_af = bufs("afd", NSEG)
    B_cf = bufs("cfd", NSEG)
    B_og = bufs("ogd", NSEG)
    B_rope = Buf("rope")
    B_kloc = [bufs(f"kloc{l}_", 8) for l in range(L)]
    B_kmid = [bufs(f"kmid{l}_", 8) for l in range(L)]
    B_kall = [bufs(f"kall{l}_", 8) for l in range(L)]
    B_vloc = [bufs(f"vloc{l}_", 8) for l in range(L)]
    B_vmid = [bufs(f"vmid{l}_", 8) for l in range(L)]
    B_vall = [bufs(f"vall{l}_", 8) for l in range(L)]
    B_hloc = bufs("hloc", L)
    B_hmid = bufs("hmid", L)
    B_hall = bufs("hall", L)

    with ExitStack() as gst:
        P = Prog(nc, gst)

        uid = [0]

        def sb(stk, name, shape, dt):
            uid[0] += 1
            return stk.enter_context(nc.sbuf_tensor(f"{name}_u{uid[0]}", list(shape), dt))

        def ACT(out, in_, func, reads, writes, **kw):
            return P.op("act", lambda e: e.activation(out=out, in_=in_, func=func, **kw), reads, writes)

        def TT(eng, out, a, b, op, reads, writes):
            return P.op(eng, lambda e: e.tensor_tensor(out=out, in0=a, in1=b, op=op), reads, writes)

        def TS(eng, out, a, s1, s2, op0, op1, reads, writes):
            if s2 is None:
                return P.op(eng, lambda e: e.tensor_scalar(out=out, in0=a, scalar1=s1, scalar2=None, op0=op0),
                            reads, writes)
            return P.op(eng, lambda e: e.tensor_scalar(out=out, in0=a, scalar1=s1, scalar2=s2, op0=op0, op1=op1),
                        reads, writes)

        def STT(out, in0, scalar, in1, op0, op1, reads, writes):
            return P.op("dve", lambda e: e.scalar_tensor_tensor(out=out, in0=in0, scalar=scalar, in1=in1,
                                                                op0=op0, op1=op1), reads, writes)

        def CP(eng, out, in_, reads, writes):
            return P.op(eng, lambda e: e.tensor_copy(out, in_), reads, writes)

        def MSET(eng, ap, val, writes):
            return P.op(eng, lambda e: e.memset(ap, val), (), writes)

        def MM(mms, reads, writes):
            def fn(e, mms=tuple(mms)):
                ins = None
                for (o, lt, rh, st, sp_) in mms:
                    ins = e.matmul(o, lt, rh, start=st, stop=sp_)
                return ins
            return P.op("pe", fn, reads, writes)

        ones_f = sb(gst, "ones_f", [128, 128], F32)
        ones_b = sb(gst, "ones_b", [128, 128], BF16)
        mhalf = sb(gst, "mhalf", [128, BLK], F32)
        prm = sb(gst, "prm_sb", [128, NPRM], F32)
        sgw_f = sb(gst, "sgw_f", [128, L * 4 * 128], F32)
        sgw_b = sb(gst, "sgw_b", [128, L * 4 * 128], BF16)
        lam_sb = sb(gst, "lam_sb", [128, 4], F32)
        ps = gst.enter_context(nc.psum_tensor("ps", [128, 8 * 512], F32))
        BK = bufs("bank", 8)
        B_const = Buf("const")
        B_prm = Buf("prm")

        def bank(i):
            return ps[:, i * 512:(i + 1) * 512]

        def pcol(o, n=1):
            return prm[:, o:o + n]

        MSET("dve", ones_f[:], 1.0, [B_const])
        MSET("dve", ones_b[:], 1.0, [B_const])
        MSET("pool", mhalf[:], -0.5, [B_const])
        P.dma(prm[:], prm_d, writes=[B_prm])
        P.dma(sgw_f[:], sgw_d, writes=[B_prm])
        CP("pool", sgw_b[:], sgw_f[:], [B_prm], [B_const])

        with ExitStack() as st:
            lt = sb(st, "lam_t", [128, 8, 64], F32)
            ls = sb(st, "lam_s", [128, 8], F32)
            B_lt = Buf("lt")
            for l in range(L):
                for m in range(2):
                    o = O_LAMV + (l * 4 + 2 * m) * 64
                    TT("dve", lt[:, l * 2 + m, :], prm[:, o:o + 64], prm[:, o + 64:o + 128], ALU.mult,
                       [B_prm], [B_lt])
                    P.op("dve", lambda e, a=ls[:, l * 2 + m:l * 2 + m + 1], b=lt[:, l * 2 + m, :]:
                         e.reduce_sum(out=a, in_=b, axis=AX.X), [B_lt], [B_lt])
            ACT(ls[:, 4:8], ls[:, 0:4], AF.Exp, [B_lt], [B_lt])
            for l in range(L):
                li = lambda_init_fn(l)
                TT("dve", lam_sb[:, l:l + 1], ls[:, 4 + 2 * l:5 + 2 * l], ls[:, 5 + 2 * l:6 + 2 * l], ALU.subtract,
                   [B_lt], [B_const])
                TS("dve", lam_sb[:, l:l + 1], lam_sb[:, l:l + 1], float(li), None, ALU.add, None, [B_const], [B_const])
                TS("dve", lam_sb[:, 2 + l:3 + l], pcol(O_SUBG + l), float(1.0 - li), None, ALU.mult, None,
                   [B_prm], [B_const])

            posb = sb(st, "posb", [128, BLK], F32)
            ang = sb(st, "ang", [128, BLK], F32)
            uu = sb(st, "uu", [128, BLK], F32)
            qq = sb(st, "qq", [128, BLK], F32)
            qi = sb(st, "qi", [128, BLK], I32)
            so = sb(st, "so", [2, 128, BLK], F32) if False else None
            so0 = sb(st, "so0", [128, BLK], F32)
            so1 = sb(st, "so1", [128, BLK], F32)
            B_pos, B_ang, B_uu, B_qq, B_qi = Buf("pos"), Buf("ang"), Buf("uu"), Buf("qq"), Buf("qi")
            B_so = bufs("so", 2)
            for pb in range(2 * SEG // BLK):
                cs = slice(pb * BLK, (pb + 1) * BLK)
                P.dma(posb[:], pos_d[:, cs], writes=[B_pos])
                TS("dve", ang[:], posb[:], pcol(O_INVF), None, ALU.mult, None, [B_pos, B_prm], [B_ang])
                for which, (shift, sot, dst) in enumerate(((0.0, so0, ropeS), (float(np.pi / 2), so1, ropeC))):
                    TS("dve", uu[:], ang[:], shift, None, ALU.add, None, [B_ang], [B_uu])
                    TS("dve", qq[:], uu[:], 1.0 / TWO_PI, None, ALU.mult, None, [B_uu], [B_qq])
                    CP("dve", qi[:], qq[:], [B_qq], [B_qi])
                    CP("dve", qq[:], qi[:], [B_qi], [B_qq])
                    STT(uu[:], qq[:], -TWO_PI, uu[:], ALU.mult, ALU.add, [B_qq, B_uu], [B_uu])
                    TS("dve", qq[:], uu[:], float(np.pi), None, ALU.is_gt, None, [B_uu], [B_qq])
                    STT(uu[:], qq[:], -TWO_PI, uu[:], ALU.mult, ALU.add, [B_qq, B_uu], [B_uu])
                    TS("dve", qq[:], uu[:], float(-np.pi), None, ALU.is_lt, None, [B_uu], [B_qq])
                    STT(uu[:], qq[:], TWO_PI, uu[:], ALU.mult, ALU.add, [B_qq, B_uu], [B_uu])
                    ACT(sot[:], uu[:], AF.Sin, [B_uu], [B_so[which]])
                    if which == 0:
                        TS("dve", sot[:], sot[:], pcol(O_SGN), None, ALU.mult, None, [B_so[0], B_prm], [B_so[0]])
                    P.dma(dst[:, cs], sot[:], reads=[B_so[which]], writes=[B_rope], q="pool")
            P.barrier()
        if stop_after == "setup":
            return _finish(nc, P, yT, None)

        def rstd_from_bank(bk, bkbuf, n_feat, out_ap, out_buf, tmp_ap, tmp_buf):
            TS("dve", tmp_ap, bk, 1.0 / n_feat, EPS, ALU.mult, ALU.add, [bkbuf], [tmp_buf])
            ACT(tmp_ap, tmp_ap, AF.Ln, [tmp_buf], [tmp_buf])
            ACT(out_ap, tmp_ap, AF.Exp, [tmp_buf], [out_buf], scale=-0.5)

        def norm_pass(stk, src, src_bufs, dst, dst_bufs, gcol, tag, dst_kind):
            pass

        for l in range(L):
            xsrc, xsrc_b = (xT, B_x["x0"]) if l == 0 else (x1T, B_x["x1"])
            xdst, xdst_b = (x1T, B_x["x1"]) if l == 0 else (x2T, B_x["x2"])
            w_in_l = w_in[l].rearrange("(k p) c -> p k c", p=128)
            for s in range(NSEG):
                t0 = s * SEG
                rope0 = 0 if s == 0 else SEG
                with ExitStack() as p1:
                    hT = sb(p1, "hT", [128, KC, SEG], BF16)
                    B_h = [[Buf(f"h{k}_{b}") for b in range(NB)] for k in range(KC)]
                    with ExitStack() as st:
                        xb = [sb(st, f"xb{i}", [128, KC, BLK], F32) for i in range(2)]
                        sq = [sb(st, f"sq{i}", [128, BLK], F32) for i in range(2)]
                        rs = [sb(st, f"rs{i}", [128, BLK], F32) for i in range(2)]
                        rt = [sb(st, f"rt{i}", [128, BLK], F32) for i in range(2)]
                        B_xb, B_sq, B_rs, B_rt = bufs("xb", 2), bufs("sq", 2), bufs("rs", 2), bufs("rt", 2)
                        for b in range(NB):
                            i = b % 2
                            tsl = slice(t0 + b * BLK, t0 + (b + 1) * BLK)
                            P.dma(xb[i][:], xsrc.rearrange("(k p) t -> p k t", p=128)[:, :, tsl],
                                  reads=[xsrc_b[s]], writes=[B_xb[i]])
                            bki = 4 + i
                            for k in range(KC):
                                j = k % 2
                                ACT(sq[j][:], xb[i][:, k, :], AF.Square, [B_xb[i]], [B_sq[j]])
                                MM([(bank(bki), ones_f[:], sq[j][:], k == 0, k == KC - 1)], [B_sq[j], B_const],
                                   [BK[bki]])
                            rstd_from_bank(bank(bki), BK[bki], D, rs[i][:], B_rs[i], rt[i][:], B_rt[i])
                            for k in range(KC):
                                STT(hT[:, k, b * BLK:(b + 1) * BLK], xb[i][:, k, :], pcol(O_GN + l * 8 + k),
                                    rs[i][:], ALU.mult, ALU.mult, [B_xb[i], B_rs[i], B_prm], [B_h[k][b]])
                        P.dma(hT_d[s].rearrange("(k p) t -> p k t", p=128), hT[:], reads=[B_h], writes=[B_hT[s]],
                              q="pool")
                        P.barrier()
                    if stop_after == ("p1a", l, s):
                        return _finish(nc, P, yT, None)

                    wst = [sb(p1, f"wst{i}", [128, KC, 256], F32) for i in range(2)]
                    wbf = [sb(p1, f"wbf{i}", [128, KC, 256], BF16) for i in range(2)]
                    B_wst, B_wbf = bufs("wst", 2), bufs("wbf", 2)
                    a_sb = sb(p1, "a_sb", [128, 4, SEG + 2 * HALO], BF16)
                    sga = sb(p1, "sga", [128, 4, SEG], BF16)
                    B_a = [[Buf(f"a{i}_{b}") for b in range(NB)] for i in range(4)]
                    B_ahalo = Buf("ahalo")
                    B_sga = [[Buf(f"sga{i}_{b}") for b in range(NB)] for i in range(4)]
                    wctr = [0]
                    pctr = [0]
                    pend1, pend2, pend2_ready = [], [], []

                    def flush_cc(final=False):
                        while pend1:
                            a_, b_, g_, rb, wb = pend1.pop(0)
                            P.collective(a_, b_, g_, reads=[rb], writes=[wb])
                            pend2_ready.append(pend2.pop(0))
                        if final:
                            while pend2_ready:
                                a_, b_, g_, rb, wb = pend2_ready.pop(0)
                                P.collective(a_, b_, g_, reads=[rb], writes=[wb])

                    def load_w(chunks, swap=False):
                        flush_cc()
                        i = wctr[0] % 2
                        wctr[0] += 1
                        for j, c in enumerate(chunks):
                            P.dma(wst[i][:, :, j * 128:(j + 1) * 128], w_in_l[:, :, c * 128:(c + 1) * 128],
                                  writes=[B_wst[i]] if j == 0 else [B_wst[i]])
                        n = len(chunks) * 128
                        if not swap:
                            ACT(wbf[i][:, :, 0:n], wst[i][:, :, 0:n], AF.Copy, [B_wst[i]], [B_wbf[i]])
                        else:
                            ACT(wbf[i][:, :, 0:128], wst[i][:, :, 0:128], AF.Copy, [B_wst[i]], [B_wbf[i]])
                            src = wst[i][:, :, 0:128].rearrange("p k (c h j) -> p k c h j", c=2, h=2)
                            dst = wbf[i][:, :, 128:256].rearrange("p k (c h j) -> p k c h j", c=2, h=2)
                            for c2 in range(2):
                                CP("dve", dst[:, :, c2, 0, :], src[:, :, c2, 1, :], [B_wst[i]], [B_wbf[i]])
                                CP("dve", dst[:, :, c2, 1, :], src[:, :, c2, 0, :], [B_wst[i]], [B_wbf[i]])
                        return wbf[i], B_wbf[i]

                    def fm_job(wt, wtb, nout, consumer):
                        for b in range(NB):
                            pr = pctr[0] % 2
                            pctr[0] += 1
                            bks = [2 * pr + o for o in range(nout)]
                            for o in range(nout):
                                MM([(bank(bks[o]), wt[:, k, o * 128:(o + 1) * 128], hT[:, k, b * BLK:(b + 1) * BLK],
                                     k == 0, k == KC - 1) for k in range(KC)],
                                   [wtb] + [B_h[k][b] for k in range(KC)], [BK[bks[o]]])
                            consumer(b, bks)

                    with ExitStack() as sA:
                        tA = [sb(sA, f"tA{i}", [128, BLK], F32) for i in range(2)]
                        B_tA = bufs("tA", 2)
                        tctr = [0]
                        for i4 in range(4):
                            wt, wtb = load_w([CH_AIN + i4, CH_AIN + 4 + i4])

                            def cons(b, bks, i4=i4):
                                ti = tctr[0] % 2
                                tctr[0] += 1
                                ACT(tA[ti][:], bank(bks[1]), AF.Sigmoid, [BK[bks[1]]], [B_tA[ti]])
                                TT("dve", a_sb[:, i4, HALO + b * BLK:HALO + (b + 1) * BLK], bank(bks[0]), tA[ti][:],
                                   ALU.mult, [BK[bks[0]], B_tA[ti]], [B_a[i4][b]])
                            fm_job(wt, wtb, 2, cons)
                        if s == 0:
                            P.dma(hl_loc[l].rearrange("(i p) c -> p i c", p=128)[:, :, 0:HALO],
                                  a_sb[:, :, HALO:2 * HALO], reads=[B_a], writes=[B_hloc[l]], q="pool")
                            P.dma(hl_loc[l].rearrange("(i p) c -> p i c", p=128)[:, :, HALO:2 * HALO],
                                  a_sb[:, :, SEG:SEG + HALO], reads=[B_a], writes=[B_hloc[l]], q="pool")
                            P.collective(hl_loc[l], hl_mid[l], G4, reads=[B_hloc[l]], writes=[B_hmid[l]])
                            P.collective(hl_mid[l], hl_all[l], G2, reads=[B_hmid[l]], writes=[B_hall[l]])
                        else:
                            MSET("pool", a_sb[:, :, 0:HALO], 0.0, [B_ahalo])
                            MSET("pool", a_sb[:, :, SEG + HALO:SEG + 2 * HALO], 0.0, [B_ahalo])
                        for j2 in range(2):
                            wt, wtb = load_w([CH_AGATE + 2 * j2, CH_AGATE + 2 * j2 + 1])

                            def cons(b, bks, j2=j2):
                                for o in range(2):
                                    ACT(sga[:, 2 * j2 + o, b * BLK:(b + 1) * BLK], bank(bks[o]), AF.Silu,
                                        [BK[bks[o]]], [B_sga[2 * j2 + o][b]])
                            fm_job(wt, wtb, 2, cons)
                        P.barrier()
                    if stop_after == ("p1A", l, s):
                        return _finish(nc, P, yT, None)

                    with ExitStack() as sB:
                        cosb = sb(sB, "cosb", [128, SEG], F32)
                        sinb = sb(sB, "sinb", [128, SEG], F32)
                        B_cs = Buf("cs")
                        P.dma(cosb[:], ropeC[:, rope0:rope0 + SEG], reads=[B_rope], writes=[B_cs])
                        P.dma(sinb[:], ropeS[:, rope0:rope0 + SEG], reads=[B_rope], writes=[B_cs])
                        qst = [sb(sB, f"qst{i}", [128, SEG], BF16) for i in range(2)]
                        B_qst = [[Buf(f"qst{i}_{b}") for b in range(NB)] for i in range(2)]
                        t1 = [sb(sB, f"t1_{i}", [128, BLK], F32) for i in range(2)]
                        t2 = [sb(sB, f"t2_{i}", [128, BLK], F32) for i in range(2)]
                        B_t1, B_t2 = bufs("t1", 2), bufs("t2", 2)
                        vst = [sb(sB, f"vst{i}", [128, SEG // 128, 256], BF16) for i in range(2)]
                        B_vst = [[Buf(f"vst{i}_{tt}") for tt in range(SEG // 128)] for i in range(2)]
                        qctr = [0]
                        tctr = [0]
                        for which in range(2):
                            for h in range(8):
                                wt, wtb = load_w([(CH_Q if which == 0 else CH_K) + h], swap=True)
                                qi_ = qctr[0] % 2
                                qctr[0] += 1

                                def cons(b, bks, qi_=qi_):
                                    ti = tctr[0] % 2
                                    tctr[0] += 1
                                    bs = slice(b * BLK, (b + 1) * BLK)
                                    TT("dve", t1[ti][:], bank(bks[0]), cosb[:, bs], ALU.mult, [BK[bks[0]], B_cs],
                                       [B_t1[ti]])
                                    TT("dve", t2[ti][:], bank(bks[1]), sinb[:, bs], ALU.mult, [BK[bks[1]], B_cs],
                                       [B_t2[ti]])
                                    TT("pool", qst[qi_][:, bs], t1[ti][:], t2[ti][:], ALU.add, [B_t1[ti], B_t2[ti]],
                                       [B_qst[qi_][b]])
                                fm_job(wt, wtb, 2, cons)
                                if which == 0:
                                    P.dma(qT_d[s][h * 128:(h + 1) * 128, :], qst[qi_][:], reads=[B_qst[qi_]],
                                          writes=[B_qT[s]], q="pool")
                                elif s == 0:
                                    P.dma(kT_loc[l][h], qst[qi_][:], reads=[B_qst[qi_]], writes=[B_kloc[l][h]], q="pool")
                                    pend1.append((kT_loc[l][h], kT_mid[l][h], G4, B_kloc[l][h], B_kmid[l][h]))
                                    pend2.append((kT_mid[l][h], kT_all[l][h], G2, B_kmid[l][h], B_kall[l][h]))
                                else:
                                    P.dma(kTs_d[s - 1][h * 128:(h + 1) * 128, :], qst[qi_][:], reads=[B_qst[qi_]],
                                          writes=[B_kTs[s - 1]], q="pool")
                        for vj in range(4):
                            wt, wtb = load_w([CH_V + 2 * vj, CH_V + 2 * vj + 1])
                            vi = vj % 2
                            for tt in range(SEG // 128):
                                bk = pctr[0] % 4
                                pctr[0] += 1
                                MM([(ps[:, bk * 512:bk * 512 + 256], hT[:, k, tt * 128:(tt + 1) * 128], wt[:, k, 0:256],
                                     k == 0, k == KC - 1) for k in range(KC)],
                                   [wtb] + [B_h[k][tt // 4] for k in range(KC)], [BK[bk]])
                                if tt % 2 == 0:
                                    ACT(vst[vi][:, tt, :], ps[:, bk * 512:bk * 512 + 256], AF.Copy, [BK[bk]],
                                        [B_vst[vi][tt]])
                                else:
                                    CP("dve", vst[vi][:, tt, :], ps[:, bk * 512:bk * 512 + 256], [BK[bk]],
                                       [B_vst[vi][tt]])
                            for hh in range(2):
                                h = 2 * vj + hh
                                if s == 0:
                                    P.dma(v_loc[l][h].rearrange("(n p) c -> p n c", p=128),
                                          vst[vi][:, :, hh * 128:(hh + 1) * 128], reads=[B_vst[vi]],
                                          writes=[B_vloc[l][h]], q="pool")
                                    pend1.append((v_loc[l][h], v_mid[l][h], G4, B_vloc[l][h], B_vmid[l][h]))
                                    pend2.append((v_mid[l][h], v_all[l][h], G2, B_vmid[l][h], B_vall[l][h]))
                                else:
                                    P.dma(vs_d[s - 1][h].rearrange("(n p) c -> p n c", p=128),
                                          vst[vi][:, :, hh * 128:(hh + 1) * 128], reads=[B_vst[vi]],
                                          writes=[B_vs[s - 1][h]], q="pool")
                        for j4 in range(4):
                            wt, wtb = load_w([CH_BG + 2 * j4, CH_BG + 2 * j4 + 1])
                            qis = []
                            for o in range(2):
                                qis.append(qctr[0] % 2)
                                qctr[0] += 1

                            def cons(b, bks, qis=qis):
                                for o in range(2):
                                    ACT(qst[qis[o]][:, b * BLK:(b + 1) * BLK], bank(bks[o]), AF.Silu, [BK[bks[o]]],
                                        [B_qst[qis[o]][b]])
                            fm_job(wt, wtb, 2, cons)
                            for o in range(2):
                                c = 2 * j4 + o
                                P.dma(gb_d[s][c * 128:(c + 1) * 128, :], qst[qis[o]][:], reads=[B_qst[qis[o]]],
                                      writes=[B_gb[s]], q="pool")
                        P.barrier()
                    if stop_after == ("p1B", l, s):
                        return _finish(nc, P, yT, None)

                    with ExitStack() as sA2:
                        c_sb = sb(sA2, "c_sb", [128, 4, SEG], F32)
                        B_c = [[Buf(f"c{i}_{b}") for b in range(NB)] for i in range(4)]
                        if s == 0:
                            hall = sb(sA2, "hall", [128, 32, 32], BF16)
                            B_hl = Buf("hl")
                            P.dma(hall[:], hl_all[l].rearrange("(r p) c -> p r c", p=128), reads=[B_hall[l]],
                                  writes=[B_hl])
                            MSET("dve", a_sb[:, :, 0:HALO], 0.0, [B_ahalo])
                            MSET("dve", a_sb[:, :, SEG + HALO:SEG + 2 * HALO], 0.0, [B_ahalo])
                            for r in range(8):
                                STT(a_sb[:, :, 0:HALO], hall[:, 4 * r:4 * r + 4, HALO:2 * HALO], pcol(O_SEL + r),
                                    a_sb[:, :, 0:HALO], ALU.mult, ALU.add, [B_hl, B_prm, B_ahalo], [B_ahalo])
                                STT(a_sb[:, :, SEG + HALO:SEG + 2 * HALO], hall[:, 4 * r:4 * r + 4, 0:HALO],
                                    pcol(O_SEL + 8 + r), a_sb[:, :, SEG + HALO:SEG + 2 * HALO], ALU.mult, ALU.add,
                                    [B_hl, B_prm, B_ahalo], [B_ahalo])
                        for i4 in range(4):
                            cwb = O_CW + (l * 4 + i4) * CONV_K
                            TS("dve", c_sb[:, i4, :], a_sb[:, i4, 0:SEG], pcol(cwb), pcol(O_CB + l * 4 + i4),
                               ALU.mult, ALU.add, [B_a[i4], B_ahalo, B_prm], [B_c[i4]])
                            for j in range(1, CONV_K):
                                STT(c_sb[:, i4, :], a_sb[:, i4, j:j + SEG], pcol(cwb + j), c_sb[:, i4, :], ALU.mult,
                                    ALU.add, [B_a[i4], B_ahalo, B_prm, B_c[i4]], [B_c[i4]])
                        sq = [sb(sA2, f"sqA{i}", [128, BLK], F32) for i in range(2)]
                        rs = [sb(sA2, f"rsA{i}", [128, BLK], F32) for i in range(2)]
                        rt = [sb(sA2, f"rtA{i}", [128, BLK], F32) for i in range(2)]
                        an = [sb(sA2, f"anA{i}", [128, BLK], F32) for i in range(2)]
                        sn = [sb(sA2, f"snA{i}", [128, BLK], F32) for i in range(2)]
                        B_sq, B_rs, B_rt, B_an, B_sn = bufs("sqA", 2), bufs("rsA", 2), bufs("rtA", 2), bufs("anA", 2), \
                            bufs("snA", 2)
                        cnt = 0
                        for b in range(NB):
                            bs = slice(b * BLK, (b + 1) * BLK)
                            bki = 4 + b % 2
                            ri = b % 2
                            for i4 in range(4):
                                j = cnt % 2
                                cnt += 1
                                ACT(sq[j][:], c_sb[:, i4, bs], AF.Square, [B_c[i4][b]], [B_sq[j]])
                                MM([(bank(bki), ones_f[:], sq[j][:], i4 == 0, i4 == 3)], [B_sq[j], B_const], [BK[bki]])
                            rstd_from_bank(bank(bki), BK[bki], 512, rs[ri][:], B_rs[ri], rt[ri][:], B_rt[ri])
                            for i4 in range(4):
                                j = cnt % 2
                                cnt += 1
                                STT(an[j][:], c_sb[:, i4, bs], pcol(O_CG + l * 4 + i4), rs[ri][:], ALU.mult, ALU.mult,
                                    [B_c[i4][b], B_rs[ri], B_prm], [B_an[j]])
                                ACT(sn[j][:], an[j][:], AF.Silu, [B_an[j]], [B_sn[j]])
                                TT("pool", a_sb[:, i4, HALO + b * BLK:HALO + (b + 1) * BLK], sn[j][:], sga[:, i4, bs],
                                   ALU.mult, [B_sn[j], B_sga[i4][b]], [B_a[i4][b]])
                        P.dma(af_d[s].rearrange("(i p) t -> p i t", p=128), a_sb[:, :, HALO:HALO + SEG], reads=[B_a],
                              writes=[B_af[s]], q="pool")
                        P.barrier()
                    if stop_after == ("p1A2", l, s):
                        return _finish(nc, P, yT, None)

                    with ExitStack() as sC:
                        cug = sb(sC, "cug", [128, 4, SEG], BF16)
                        B_cug = [[Buf(f"cug{i}_{tt}") for tt in range(SEG // 128)] for i in range(4)]
                        tC = [sb(sC, f"tC{i}", [128, BLK], F32) for i in range(2)]
                        B_tC = bufs("tC", 2)
                        tctr = [0]
                        for i4 in range(4):
                            wt, wtb = load_w([CH_CU + i4, CH_CG + i4])

                            def cons(b, bks, i4=i4):
                                ti = tctr[0] % 2
                                tctr[0] += 1
                                ACT(tC[ti][:], bank(bks[1]), AF.Silu, [BK[bks[1]]], [B_tC[ti]])
                                TT("dve", cug[:, i4, b * BLK:(b + 1) * BLK], bank(bks[0]), tC[ti][:], ALU.mult,
                                   [BK[bks[0]], B_tC[ti]], [B_cug[i4][4 * b:4 * b + 4]])
                            fm_job(wt, wtb, 2, cons)
                        if stop_after == ("p1C1", l, s):
                            return _finish(nc, P, yT, None)
                        sqv = [sb(sC, f"sqv{i}", [128, BLK], F32) for i in range(2)]
                        ssv = [sb(sC, f"ssv{i}", [128, 2], F32) for i in range(2)]
                        cvn = [sb(sC, f"cvn{i}", [128, BLK], BF16) for i in range(2)]
                        tmx = [sb(sC, f"tmx{i}", [128, BLK], F32) for i in range(2)]
                        B_sqv, B_ssv, B_cvn, B_tmx = bufs("sqv", 2), bufs("ssv", 2), bufs("cvn", 2), bufs("tmx", 2)
                        wts = []
                        for j2 in range(2):
                            wts.append(load_w([CH_CV + 2 * j2, CH_CV + 2 * j2 + 1]))
                        import os
                        for tt in range(int(os.environ.get("K_NTT", SEG // 128))):
                            i = tt % 2
                            bk = pctr[0] % 4
                            pctr[0] += 1
                            mms = []
                            for j2 in range(2):
                                wt, wtb = wts[j2]
                                mms += [(ps[:, bk * 512 + j2 * 256:bk * 512 + (j2 + 1) * 256],
                                         hT[:, k, tt * 128:(tt + 1) * 128], wt[:, k, 0:256], k == 0, k == KC - 1)
                                        for k in range(KC)]
                            MM(mms, [wts[0][1], wts[1][1]] + [B_h[k][tt // 4] for k in range(KC)], [BK[bk]])
                            ACT(sqv[i][:], bank(bk), AF.Square, [BK[bk]], [B_sqv[i]])
                            P.op("dve", lambda e, a=ssv[i][:, 0:1], b_=sqv[i][:]: e.reduce_sum(out=a, in_=b_, axis=AX.X),
                                 [B_sqv[i]], [B_ssv[i]])
                            TS("dve", ssv[i][:, 0:1], ssv[i][:, 0:1], 1.0 / 512, EPS, ALU.mult, ALU.add, [B_ssv[i]],
                               [B_ssv[i]])
                            ACT(ssv[i][:, 0:1], ssv[i][:, 0:1], AF.Ln, [B_ssv[i]], [B_ssv[i]])
                            ACT(ssv[i][:, 1:2], ssv[i][:, 0:1], AF.Exp, [B_ssv[i]], [B_ssv[i]], scale=-0.5)
                            STT(cvn[i][:], bank(bk), ssv[i][:, 1:2], prm[:, O_SGNG + l * 512:O_SGNG + (l + 1) * 512],
                                ALU.mult, ALU.mult, [BK[bk], B_ssv[i], B_prm], [B_cvn[i]])
                            mb = 4 + tt % 2
                            MM([(ps[:, mb * 512 + g * 128:mb * 512 + (g + 1) * 128], cvn[i][:, g * 128:(g + 1) * 128],
                                 sgw_b[:, (l * 4 + g) * 128:(l * 4 + g + 1) * 128], True, True) for g in range(4)],
                               [B_cvn[i], B_const], [BK[mb]])
                            TT("dve", tmx[i][:], bank(mb), prm[:, O_SGUB + l * 512:O_SGUB + (l + 1) * 512], ALU.add,
                               [BK[mb], B_prm], [B_tmx[i]])
                            if os.environ.get("K_CSTOP") == "1":
                                continue
                            cview = cug[:, :, tt * 128:(tt + 1) * 128]
                            TT(os.environ.get("K_CENG", "dve"), cview, tmx[i][:].rearrange("p (g q) -> p g q", g=4), cview, ALU.mult,
                               [B_tmx[i]] + [B_cug[g][tt] for g in range(4)], [B_cug[g][tt] for g in range(4)])
                        P.dma(cf_d[s].rearrange("(i p) t -> p i t", p=128), cug[:], reads=[B_cug], writes=[B_cf[s]],
                              q="pool")
                        flush_cc(final=True)
                        P.barrier()
                    P.barrier()
                if stop_after == ("p1", l, s):
                    return _finish(nc, P, yT, None)

            with ExitStack() as p2:
                KT = sb(p2, "KT", [128, 8 * SEG], BF16)
                VT = sb(p2, "VT", [128, 8 * SEG // 128, 128], BF16)
                B_KT, B_VT = bufs("KT", 8), bufs("VT", 8)
                QT = [sb(p2, f"QT{i}", [128, SEG], BF16) for i in range(2)]
                GB = [sb(p2, f"GB{i}", [128, SEG], BF16) for i in range(2)]
                B_QT, B_GB = bufs("QT", 2), bufs("GB", 2)
                NPT, NPS = 6, 4
                PT = [sb(p2, f"PT{i}", [128, 1024], BF16) for i in range(NPT)]
                B_PT = bufs("PT", NPT)
                PQ = [sb(p2, f"PQ{i}", [128, 1024], BF16) for i in range(2)]
                B_PQ = bufs("PQ", 2)
                PS = [sb(p2, f"PS{i}", [128, 1024], BF16) for i in range(NPS)]
                B_PS = bufs("PS", NPS)
                ogst = [sb(p2, f"ogst{i}", [128, SEG], BF16) for i in range(2)]
                B_ogst = [[Buf(f"ogst{i}_{b}") for b in range(NB)] for i in range(2)]
                ep = {nm: [sb(p2, f"ep_{nm}{i}", [128, BLK], F32) for i in range(2)]
                      for nm in ("c1", "c2", "r1", "r2", "o1", "o2", "os", "sq", "rt", "rs", "on")}
                B_ep = {nm: bufs("ep_" + nm, 2) for nm in ep}
                hctr = 0
                pctr2 = 0
                ectr = 0
                gctr = 0
                MSET("dve", PQ[1][:], 0.0, [B_PQ[1]])
                deferred = []
                DEF_J, SUM_LAG = 12, 11
                for s in range(NSEG):
                    nkc = 8 if s == 0 else 1
                    for h in range(8):
                        hi = hctr % 2
                        hctr += 1
                        if s == 0:
                            kv = kT_all[l][h].rearrange("(r p) t -> p r t", p=128)
                            vv = v_all[l][h].rearrange("(r n p) d -> p r n d", r=8, p=128)
                            for r in range(8):
                                P.dma(KT[:, r * SEG:(r + 1) * SEG], kv[:, r, :], reads=[B_kall[l][h]],
                                      writes=[B_KT[r]])
                            for r in range(8):
                                P.dma(VT[:, r * 16:(r + 1) * 16, :], vv[:, r, :, :], reads=[B_vall[l][h]],
                                      writes=[B_VT[r]])
                        else:
                            P.dma(KT[:, 0:SEG], kTs_d[s - 1][h * 128:(h + 1) * 128, :], reads=[B_kTs[s - 1]],
                                  writes=[B_KT[0]])
                            P.dma(VT[:, 0:16, :], vs_d[s - 1][h].rearrange("(n p) d -> p n d", p=128),
                                  reads=[B_vs[s - 1][h]], writes=[B_VT[0]])
                        P.dma(QT[hi][:], qT_d[s][h * 128:(h + 1) * 128, :], reads=[B_qT[s]], writes=[B_QT[hi]])
                        P.dma(GB[hi][:], gb_d[s][h * 128:(h + 1) * 128, :], reads=[B_gb[s]], writes=[B_GB[hi]])
                        nt = nkc * 16
                        ng = nt // 4
                        for qb in range(NB):
                            qs = slice(qb * BLK, (qb + 1) * BLK)

                            def qk(j):
                                pr = (pctr2 + j) % 2
                                ks = slice(j * 128, (j + 1) * 128)
                                MM([(ps[:, pr * 1024:pr * 1024 + 512], KT[0:64, ks], QT[hi][0:64, qs], True, True),
                                    (ps[:, pr * 1024 + 512:pr * 1024 + 1024], KT[64:128, ks], QT[hi][64:128, qs], True,
                                     True)], [B_KT[j // 16], B_QT[hi]], [BK[2 * pr], BK[2 * pr + 1]])
                            qk(0)
                            qk(1)
                            pend = []
                            for j in range(nt):
                                pr = (pctr2 + j) % 2
                                pi = (pctr2 + j) % NPT
                                pim = (pctr2 + j - 1) % NPT
                                ACT(PT[pi][:], ps[:, pr * 1024:(pr + 1) * 1024], AF.Exp, [BK[2 * pr], BK[2 * pr + 1]],
                                    [B_PT[pi]], scale=0.125)
                                if j + 2 < nt:
                                    qk(j + 2)
                                st_, sp_ = (j == 0), (j == nt - 1)
                                MM([(bank(4), VT[:, j, :], PT[pi][:, 0:512], st_, sp_),
                                    (bank(5), VT[:, j, :], PT[pi][:, 512:1024], st_, sp_)],
                                   [B_VT[j // 16], B_PT[pi]], [BK[4], BK[5]])
                                if j % 4 == 1:
                                    TT("dve", PQ[0][:], PT[pim][:], PT[pi][:], ALU.add, [B_PT[pim], B_PT[pi]], [B_PQ[0]])
                                if j % 4 == 3:
                                    g = j // 4
                                    gi = (gctr + g) % NPS
                                    TT("dve", PQ[1][:], PT[pim][:], PT[pi][:], ALU.add, [B_PT[pim], B_PT[pi]], [B_PQ[1]])
                                    TT("dve", PS[gi][:], PQ[0][:], PQ[1][:], ALU.add, [B_PQ[0], B_PQ[1]], [B_PS[gi]])
                                    pend.append(g)
                                while deferred and (j >= deferred[0][0] or j == nt - 1):
                                    deferred.pop(0)[1]()
                                while pend and (j >= 4 * pend[0] + 3 + SUM_LAG or j == nt - 1):
                                    g = pend.pop(0)
                                    gi = (gctr + g) % NPS
                                    MM([(bank(6), ones_b[:], PS[gi][:, 0:512], g == 0, g == ng - 1),
                                        (bank(7), ones_b[:], PS[gi][:, 512:1024], g == 0, g == ng - 1)],
                                       [B_PS[gi], B_const], [BK[6], BK[7]])
                            pctr2 += nt
                            gctr += ng
                            ei = ectr % 2
                            ectr += 1
                            E = {nm: ep[nm][ei] for nm in ep}
                            BE = {nm: B_ep[nm][ei] for nm in ep}
                            CP("dve", E["c1"][:], bank(4), [BK[4]], [BE["c1"]])
                            CP("dve", E["c2"][:], bank(5), [BK[5]], [BE["c2"]])
                            P.op("dve", lambda e, o=E["r2"][:], i_=bank(7): e.reciprocal(out=o, in_=i_), [BK[7]],
                                 [BE["r2"]])
                            STT(E["r1"][:], bank(6), lam_sb[:, l:l + 1], E["r2"][:], ALU.mult, ALU.mult,
                                [BK[6], BE["r2"], B_const], [BE["r1"]])

                            def d2(E=E, BE=BE):
                                ACT(E["o1"][:], bank(6), AF.Square, [BK[6]], [BE["o1"]], scale=1e-3)
                                import os as _os
                                eng_ = _os.environ.get("K_D2ENG", "dve")
                                TT(eng_, E["o2"][:], E["r1"][:], E["c2"][:], ALU.mult, [BE["r1"], BE["c2"]], [BE["o2"]])
                                TT(eng_, E["os"][:], E["c1"][:], E["o2"][:], ALU.subtract, [BE["c1"], BE["o2"]],
                                   [BE["os"]])

                            def d6(E=E, BE=BE):
                                ACT(E["sq"][:], E["os"][:], AF.Square, [BE["os"]], [BE["sq"]])

                            def d9(E=E, BE=BE):
                                MM([(bank(6), ones_f[:], E["sq"][:], True, True)], [BE["sq"], B_const], [BK[6]])
                                STT(E["rt"][:], bank(6), 1.0 / 128, E["o1"][:], ALU.mult, ALU.add,
                                    [BK[6], BE["o1"]], [BE["rt"]])

                            def d12(E=E, BE=BE, hi=hi, qs=qs, qb=qb, h=h, s=s):
                                ACT(E["rt"][:], E["rt"][:], AF.Ln, [BE["rt"]], [BE["rt"]])
                                ACT(E["rs"][:], E["rt"][:], AF.Exp, [BE["rt"]], [BE["rs"]], scale=-0.5)
                                STT(E["on"][:], E["os"][:], lam_sb[:, 2 + l:3 + l], E["rs"][:], ALU.mult, ALU.mult,
                                    [BE["os"], BE["rs"], B_const], [BE["on"]])
                                TT("pool", ogst[hi][:, qs], E["on"][:], GB[hi][:, qs], ALU.mult, [BE["on"], B_GB[hi]],
                                   [B_ogst[hi][qb]])
                                if qb == NB - 1:
                                    P.dma(og_d[s][h * 128:(h + 1) * 128, :], ogst[hi][:], reads=[B_ogst[hi]],
                                          writes=[B_og[s]], q="pool")
                            deferred.append((2, d2))
                            deferred.append((6, d6))
                            deferred.append((9, d9))
                            deferred.append((12, d12))
                while deferred:
                    deferred.pop(0)[1]()
                P.barrier()
            if stop_after == ("p2", l):
                return _finish(nc, P, yT, None)

            with ExitStack() as p3:
                HS = 1024
                h3 = sb(p3, "h3", [128, KC, HS], BF16)
                af3 = sb(p3, "af3", [128, 4, HS], BF16)
                cf3 = sb(p3, "cf3", [128, 4, HS], BF16)
                og3 = sb(p3, "og3", [128, KC, HS], BF16)
                B_in3 = bufs("in3_", 4)
                w3st = [sb(p3, f"w3st{i}", [128, 40, 128], F32) for i in range(2)]
                w3bf = [sb(p3, f"w3bf{i}", [128, 40, 128], BF16) for i in range(2)]
                B_w3st, B_w3bf = bufs("w3st", 2), bufs("w3bf", 2)
                y3 = sb(p3, "y3", [128, KC, HS], BF16)
                B_y3 = [[Buf(f"y3_{j}_{tb}") for tb in range(2)] for j in range(KC)]
                wost = [sb(p3, f"wost{i}", [128, KC, 128], F32) for i in range(2)]
                wobf = [sb(p3, f"wobf{i}", [128, KC, 128], BF16) for i in range(2)]
                B_wost, B_wobf = bufs("wost", 2), bufs("wobf", 2)
                mg = [sb(p3, f"mg{i}", [128, BLK], F32) for i in range(2)]
                ya = [sb(p3, f"ya{i}", [128, BLK], F32) for i in range(2)]
                yt = [sb(p3, f"yt{i}", [128, BLK], F32) for i in range(2)]
                xr = [sb(p3, f"xr{i}", [128, BLK], F32) for i in range(2)]
                xo = [sb(p3, f"xo{i}", [128, BLK], F32) for i in range(2)]
                B_mg, B_ya, B_yt, B_xr, B_xo = bufs("mg", 2), bufs("ya", 2), bufs("yt", 2), bufs("xr", 2), bufs("xo", 2)
                wa_l = w_pa[l].rearrange("(k p) c -> p k c", p=128)
                wb_l = w_pb[l].rearrange("(k p) c -> p k c", p=128)
                wc_l = w_pc[l].rearrange("(k p) c -> p k c", p=128)
                wo_l = w_out[l].rearrange("(k p) c -> p k c", p=128)
                w3c = 0
                g3 = 0
                x3 = 0
                for hs in range(T // HS):
                    s = hs // 2
                    c0 = (hs % 2) * HS
                    P.dma(h3[:], hT_d[s].rearrange("(k p) t -> p k t", p=128)[:, :, c0:c0 + HS], reads=[B_hT[s]],
                          writes=[B_in3[0]])
                    P.dma(af3[:], af_d[s].rearrange("(k p) t -> p k t", p=128)[:, :, c0:c0 + HS], reads=[B_af[s]],
                          writes=[B_in3[1]])
                    P.dma(cf3[:], cf_d[s].rearrange("(k p) t -> p k t", p=128)[:, :, c0:c0 + HS], reads=[B_cf[s]],
                          writes=[B_in3[2]])
                    P.dma(og3[:], og_d[s].rearrange("(k p) t -> p k t", p=128)[:, :, c0:c0 + HS], reads=[B_og[s]],
                          writes=[B_in3[3]])
                    for j in range(KC):
                        wi = w3c % 2
                        w3c += 1
                        js = slice(j * 128, (j + 1) * 128)
                        P.dma(w3st[wi][:, 0:4, :], wa_l[:, :, js], writes=[B_w3st[wi]])
                        P.dma(w3st[wi][:, 4:12, :], wb_l[:, :, js], writes=[B_w3st[wi]])
                        P.dma(w3st[wi][:, 12:16, :], wc_l[:, :, js], writes=[B_w3st[wi]])
                        for m in range(3):
                            mc = CH_M * 128 + m * D + j * 128
                            P.dma(w3st[wi][:, 16 + 8 * m:24 + 8 * m, :], w_in_l[:, :, mc:mc + 128],
                                  writes=[B_w3st[wi]])
                        ACT(w3bf[wi][:], w3st[wi][:], AF.Copy, [B_w3st[wi]], [B_w3bf[wi]])
                        W = w3bf[wi]
                        for tb in range(2):
                            ts_ = slice(tb * BLK, (tb + 1) * BLK)
                            yi = (j * 2 + tb) % 2
                            branches = ((af3, B_in3[1], 0, 4), (og3, B_in3[3], 4, 8), (cf3, B_in3[2], 12, 4))
                            for m, (src, srcb, wo_, nk) in enumerate(branches):
                                gi = g3 % 2
                                g3 += 1
                                gbk, ybk = gi, 2 + gi
                                MM([(bank(gbk), W[:, 16 + 8 * m + k, :], h3[:, k, ts_], k == 0, k == KC - 1)
                                    for k in range(KC)], [B_w3bf[wi], B_in3[0]], [BK[gbk]])
                                MM([(bank(ybk), W[:, wo_ + k, :], src[:, k, ts_], k == 0, k == nk - 1)
                                    for k in range(nk)], [B_w3bf[wi], srcb], [BK[ybk]])
                                ACT(mg[gi][:], bank(gbk), AF.Sigmoid, [BK[gbk]], [B_mg[gi]])
                                if m == 0:
                                    TT("dve", ya[yi][:], bank(ybk), mg[gi][:], ALU.mult, [BK[ybk], B_mg[gi]],
                                       [B_ya[yi]])
                                else:
                                    TT("dve", yt[gi][:], bank(ybk), mg[gi][:], ALU.mult, [BK[ybk], B_mg[gi]],
                                       [B_yt[gi]])
                                    if m == 1:
                                        TT("pool", ya[yi][:], ya[yi][:], yt[gi][:], ALU.add, [B_ya[yi], B_yt[gi]],
                                           [B_ya[yi]])
                                    else:
                                        TT("pool", y3[:, j, ts_], ya[yi][:], yt[gi][:], ALU.add,
                                           [B_ya[yi], B_yt[gi]], [B_y3[j][tb]])
                    for j in range(KC):
                        wi = w3c % 2
                        w3c += 1
                        js = slice(j * 128, (j + 1) * 128)
                        P.dma(wost[wi][:], wo_l[:, :, js], writes=[B_wost[wi]])
                        CP("dve", wobf[wi][:], wost[wi][:], [B_wost[wi]], [B_wobf[wi]])
                        for tb in range(2):
                            ts_ = slice(tb * BLK, (tb + 1) * BLK)
                            tg = slice(hs * HS + tb * BLK, hs * HS + (tb + 1) * BLK)
                            xi = x3 % 2
                            x3 += 1
                            bk = 4 + xi
                            P.dma(xr[xi][:], xsrc[js, tg], reads=[xsrc_b[s]], writes=[B_xr[xi]])
                            MM([(bank(bk), wobf[wi][:, k, :], y3[:, k, ts_], k == 0, k == KC - 1) for k in range(KC)],
                               [B_wobf[wi]] + [B_y3[k][tb] for k in range(KC)], [BK[bk]])
                            TT("dve", xo[xi][:], bank(bk), xr[xi][:], ALU.add, [BK[bk], B_xr[xi]], [B_xo[xi]])
                            P.dma(xdst[js, tg], xo[xi][:], reads=[B_xo[xi]], writes=[xdst_b[s]], q="pool")
                P.barrier()
            if stop_after == ("p3", l):
                return _finish(nc, P, yT, None)

        with ExitStack() as st:
            xb = [sb(st, f"fxb{i}", [128, KC, BLK], F32) for i in range(2)]
            yo = [sb(st, f"fyo{i}", [128, KC, BLK], F32) for i in range(2)]
            sq = [sb(st, f"fsq{i}", [128, BLK], F32) for i in range(2)]
            rs = [sb(st, f"frs{i}", [128, BLK], F32) for i in range(2)]
            rt = [sb(st, f"frt{i}", [128, BLK], F32) for i in range(2)]
            B_xb, B_yo, B_sq, B_rs, B_rt = bufs("fxb", 2), bufs("fyo", 2), bufs("fsq", 2), bufs("frs", 2), bufs("frt", 2)
            B_y = Buf("y")
            for b in range(T // BLK):
                i = b % 2
                s = b // NB
                tsl = slice(b * BLK, (b + 1) * BLK)
                P.dma(xb[i][:], x2T.rearrange("(k p) t -> p k t", p=128)[:, :, tsl], reads=[B_x["x2"][s]],
                      writes=[B_xb[i]])
                bki = 4 + i
                for k in range(KC):
                    j = k % 2
                    ACT(sq[j][:], xb[i][:, k, :], AF.Square, [B_xb[i]], [B_sq[j]])
                    MM([(bank(bki), ones_f[:], sq[j][:], k == 0, k == KC - 1)], [B_sq[j], B_const], [BK[bki]])
                rstd_from_bank(bank(bki), BK[bki], D, rs[i][:], B_rs[i], rt[i][:], B_rt[i])
                for k in range(KC):
                    STT(yo[i][:, k, :], xb[i][:, k, :], pcol(O_FG + k), rs[i][:], ALU.mult, ALU.mult,
                        [B_xb[i], B_rs[i], B_prm], [B_yo[i]])
                P.dma(yT.rearrange("(k p) t -> p k t", p=128)[:, :, tsl], yo[i][:], reads=[B_yo[i]], writes=[B_y],
                      q="pool")
        return _finish(nc, P, yT, None)


def _finish(nc, P, yT, _):
    P.barrier()
    with nc.Block() as block:
        P.emit(block)
    return nc


def _pack_params(core, norm_g, final_g, conv_w, conv_b, conv_norm_g, subln_g, lam_q1, lam_k1, lam_q2, lam_k2,
                 sgu_norm_g, sgu_b):
    prm = np.zeros((128, NPRM), np.float32)
    p = np.arange(128)
    for l in range(L):
        prm[:, O_GN + l * 8:O_GN + (l + 1) * 8] = norm_g[l].reshape(8, 128).T
        for i in range(4):
            prm[:, O_CW + (l * 4 + i) * CONV_K:O_CW + (l * 4 + i + 1) * CONV_K] = conv_w[l][:, i * 128:(i + 1) * 128].T
        prm[:, O_CB + l * 4:O_CB + (l + 1) * 4] = conv_b[l].reshape(4, 128).T
        prm[:, O_CG + l * 4:O_CG + (l + 1) * 4] = conv_norm_g[l].reshape(4, 128).T
        prm[:, O_SUBG + l] = subln_g[l]
        for v, arr in enumerate((lam_q1, lam_k1, lam_q2, lam_k2)):
            prm[:, O_LAMV + (l * 4 + v) * 64:O_LAMV + (l * 4 + v + 1) * 64] = arr[l][None, :]
        prm[:, O_SGNG + l * 512:O_SGNG + (l + 1) * 512] = sgu_norm_g[l][None, :]
        prm[:, O_SGUB + l * 512:O_SGUB + (l + 1) * 512] = sgu_b[l].reshape(1, 512)
    prm[:, O_FG:O_FG + 8] = final_g.reshape(8, 128).T
    half = 32
    inv = (np.float32(10000.0) ** (-(np.arange(half, dtype=np.float32) / np.float32(half)))).astype(np.float32)
    d = p % 64
    prm[:, O_INVF] = inv[d % 32]
    prm[:, O_SGN] = np.where(d < 32, -1.0, 1.0)
    if core > 0:
        prm[:, O_SEL + core - 1] = 1.0
    if core < NCORES - 1:
        prm[:, O_SEL + 8 + core + 1] = 1.0
    return prm


_NC_CACHE = {}


def make_in_maps(x_prompt, x_sample, norm_g, w_in, conv_w, conv_b, conv_norm_g, w_proj_a,
                 lam_q1, lam_k1, lam_q2, lam_k2, subln_g, w_proj_b,
                 sgu_norm_g, sgu_w, sgu_b, w_proj_c, w_out, final_g):
    f = lambda a: np.ascontiguousarray(np.asarray(a, dtype=np.float32))
    x_prompt, x_sample = f(x_prompt), f(x_sample)
    w_in, w_proj_a, w_proj_b, w_proj_c, w_out = f(w_in), f(w_proj_a), f(w_proj_b), f(w_proj_c), f(w_out)
    norm_g, final_g, conv_w, conv_b, conv_norm_g = f(norm_g), f(final_g), f(conv_w), f(conv_b), f(conv_norm_g)
    subln_g, sgu_norm_g, sgu_w, sgu_b = f(subln_g), f(sgu_norm_g), f(sgu_w), f(sgu_b)
    lam_q1, lam_k1, lam_q2, lam_k2 = f(lam_q1), f(lam_k1), f(lam_q2), f(lam_k2)
    sgw = np.ascontiguousarray(sgu_w.transpose(3, 0, 1, 2).reshape(128, L * 4 * 128))
    in_maps = []
    for c in range(NCORES):
        xs = np.concatenate([x_prompt[0, c * SEG:(c + 1) * SEG], x_sample[2 * c], x_sample[2 * c + 1]], axis=0)
        pos = np.concatenate([np.arange(c * SEG, (c + 1) * SEG), np.arange(SEG)]).astype(np.float32)
        in_maps.append({
            "xT": np.ascontiguousarray(xs.T),
            "w_in": w_in, "w_pa": w_proj_a, "w_pb": w_proj_b, "w_pc": w_proj_c, "w_out": w_out,
            "prm": _pack_params(c, norm_g, final_g, conv_w, conv_b, conv_norm_g, subln_g, lam_q1, lam_k1, lam_q2,
                                lam_k2, sgu_norm_g, sgu_b),
            "sgw": sgw,
            "pos": np.ascontiguousarray(np.broadcast_to(pos[None, :], (128, 2 * SEG))),
        })
    return in_maps


def kernel(x_prompt, x_sample, norm_g, w_in, conv_w, conv_b, conv_norm_g, w_proj_a,
           lam_q1, lam_k1, lam_q2, lam_k2, subln_g, w_proj_b,
           sgu_norm_g, sgu_w, sgu_b, w_proj_c, w_out, final_g):
    in_maps = make_in_maps(x_prompt, x_sample, norm_g, w_in, conv_w, conv_b, conv_norm_g, w_proj_a,
                           lam_q1, lam_k1, lam_q2, lam_k2, subln_g, w_proj_b,
                           sgu_norm_g, sgu_w, sgu_b, w_proj_c, w_out, final_g)
    if "nc" not in _NC_CACHE:
        _NC_CACHE["nc"] = build_nc()
    res = run_bass_kernel_spmd(_NC_CACHE["nc"], in_maps, core_ids=list(range(NCORES)))
    y_prompt = np.empty((1, NCORES * SEG, D), np.float32)
    y_sample = np.empty((2 * NCORES, SEG, D), np.float32)
    for c in range(NCORES):
        y = np.asarray(res.results[c]["yT"], dtype=np.float32).T
        y_prompt[0, c * SEG:(c + 1) * SEG] = y[0:SEG]
        y_sample[2 * c] = y[SEG:2 * SEG]
        y_sample[2 * c + 1] = y[2 * SEG:3 * SEG]
    return (y_prompt, y_sample)
```

```python
import math
from contextlib import ExitStack

import numpy as np
import concourse.bass as bass
import concourse.mybir as mybir
from concourse.bass_utils import run_bass_kernel_spmd

F32 = mybir.dt.float32
BF16 = mybir.dt.bfloat16
I32 = mybir.dt.int32
AF = mybir.ActivationFunctionType
ALU = mybir.AluOpType
AX = mybir.AxisListType

NCORES = 8
D = 1024
KC = 8
SEG = 2048
NSEG = 3
T = SEG * NSEG
BLK = 512
NB = SEG // BLK
L = 2
INC = 10240
CONV_K = 31
HALO = 15
EPS = 1e-6
TWO_PI = float(2.0 * np.pi)

CH_AIN, CH_AGATE, CH_Q, CH_K, CH_V, CH_BG, CH_CU, CH_CV, CH_CG, CH_M = 0, 8, 12, 20, 28, 36, 44, 48, 52, 56

O_GN = 0
O_FG = O_GN + L * 8
O_CW = O_FG + 8
O_CB = O_CW + L * 4 * CONV_K
O_CG = O_CB + L * 4
O_SUBG = O_CG + L * 4
O_INVF = O_SUBG + L
O_SGN = O_INVF + 1
O_SEL = O_SGN + 1
O_LAMV = O_SEL + 16
O_SGNG = O_LAMV + L * 4 * 64
O_SGUB = O_SGNG + L * 512
NPRM = O_SGUB + L * 4 * 128

DEBUG_OUT = set()

CE = ("pe", "act", "dve", "pool")
ENGS = ("pe", "act", "dve", "pool", "sp")


class Buf:
    __slots__ = ("name", "w", "r")

    def __init__(self, name):
        self.name = name
        self.w = {}
        self.r = {}


def bufs(name, n):
    return [Buf(f"{name}{i}") for i in range(n)]


def flat(x):
    if isinstance(x, Buf):
        return [x]
    out = []
    for e in x:
        out.extend(flat(e))
    return out


class Prog:
    def __init__(self, nc, stack, n_dma_sems=28):
        self.nc = nc
        self.ops = {e: [] for e in ENGS}
        self.cnt = {e: 0 for e in CE}
        self.sem = {}
        for e in CE:
            self.sem[e] = stack.enter_context(nc.semaphore("s_" + e))
        self.dsems = []
        for i in range(n_dma_sems):
            nm = f"d{i}"
            self.sem[nm] = stack.enter_context(nc.semaphore(nm))
            self.dsems.append(nm)
        self.dcnt = {nm: 0 for nm in self.dsems}
        self.drr = 0
        self.known = {e: {} for e in ENGS}
        self.stack = stack
        self.ncc = 0

    def _need(self, waits, toks):
        for s, v in toks.items():
            if waits.get(s, 0) < v:
                waits[s] = v

    def _finish_waits(self, eng, waits):
        kn = self.known[eng]
        out = []
        for s, v in waits.items():
            if eng == "pe" and s == "pe":
                continue
            if kn.get(s, 0) < v:
                kn[s] = v
                out.append((s, v))
        return out

    def _deps(self, reads, writes):
        waits = {}
        for b in flat(reads):
            self._need(waits, b.w)
        for b in flat(writes):
            self._need(waits, b.w)
            self._need(waits, b.r)
        return waits

    def _commit(self, tok, reads, writes):
        s, v = tok
        for b in flat(reads):
            if b.r.get(s, 0) < v:
                b.r[s] = v
        for b in flat(writes):
            b.w = {s: v}
            b.r = {}

    def op(self, eng, fn, reads=(), writes=()):
        waits = self._deps(reads, writes)
        wl = self._finish_waits(eng, waits)
        self.cnt[eng] += 1
        tok = (eng, self.cnt[eng])
        self.ops[eng].append((wl, fn, (eng, 1)))
        self._commit(tok, reads, writes)
        return tok

    def dma(self, out, in_, reads=(), writes=(), q="sp"):
        waits = self._deps(reads, writes)
        nm = self.dsems[self.drr % len(self.dsems)]
        self.drr += 1
        if self.dcnt[nm] > 0:
            self._need(waits, {nm: self.dcnt[nm]})
        wl = self._finish_waits(q, waits)
        self.dcnt[nm] += 16
        tok = (nm, self.dcnt[nm])
        self.ops[q].append((wl, (lambda e, o=out, i=in_: e.dma_start(out=o, in_=i)), (nm, 16)))
        self._commit(tok, reads, writes)
        return tok

    def collective(self, in_ap, out_ap, groups, reads=(), writes=()):
        import os
        if os.environ.get("K_NOCC"):
            return None
        waits = self._deps(reads, writes)
        wl = self._finish_waits("pool", waits)
        nm = f"cc{self.ncc}"
        self.ncc += 1
        self.sem[nm] = self.stack.enter_context(self.nc.semaphore(nm))

        def fn(e, i=in_ap, o=out_ap, g=groups):
            return e.collective_compute("AllGather", ALU.bypass, replica_groups=g, ins=[i.opt()], outs=[o.opt()])
        self.ops["pool"].append((wl, fn, (nm, None)))
        tok = (nm, 1)
        self._commit(tok, reads, writes)
        return tok

    def barrier(self):
        allt = {e: self.cnt[e] for e in CE if self.cnt[e] > 0}
        for nm, v in self.dcnt.items():
            if v > 0:
                allt[nm] = v
        for e in ENGS:
            wl = self._finish_waits(e, dict(allt))
            if wl:
                self.ops[e].append((wl, None, None))

    def emit(self, block):
        sem = self.sem

        def run(e, lst):
            for wl, fn, inc in lst:
                for s, v in wl:
                    e.wait_ge(sem[s], v)
                if fn is None:
                    continue
                ins = fn(e)
                if inc[1] is None:
                    ins.then_inc(sem[inc[0]])
                else:
                    ins.then_inc(sem[inc[0]], inc[1])

        @block.tensor
        def _(e):
            run(e, self.ops["pe"])

        @block.scalar
        def _(e):
            run(e, self.ops["act"])

        @block.vector
        def _(e):
            run(e, self.ops["dve"])

        @block.gpsimd
        def _(e):
            run(e, self.ops["pool"])

        @block.sync
        def _(e):
            run(e, self.ops["sp"])


def lambda_init_fn(layer_idx):
    return 0.8 - 0.6 * math.exp(-0.3 * layer_idx)


def build_nc(stop_after=None):
    nc = bass.Bass("TRN2", target_bir_lowering=False)

    def dram_in(name, shape, dt=F32):
        return nc.dram_tensor(name, list(shape), dt, kind="ExternalInput").ap()

    def scratch(name, shape, dt):
        kind = "ExternalOutput" if name in DEBUG_OUT else "Internal"
        return nc.dram_tensor(name, list(shape), dt, kind=kind).ap()

    xT = dram_in("xT", [D, T])
    w_in = dram_in("w_in", [L, D, INC])
    w_pa = dram_in("w_pa", [L, 512, D])
    w_pb = dram_in("w_pb", [L, D, D])
    w_pc = dram_in("w_pc", [L, 512, D])
    w_out = dram_in("w_out", [L, D, D])
    prm_d = dram_in("prm", [128, NPRM])
    sgw_d = dram_in("sgw", [128, L * 4 * 128])
    pos_d = dram_in("pos", [128, 2 * SEG])
    yT = nc.dram_tensor("yT", [D, T], F32, kind="ExternalOutput").ap()

    x1T = scratch("x1T", [D, T], F32)
    x2T = scratch("x2T", [D, T], F32)
    hT_d = scratch("hT_d", [NSEG, D, SEG], BF16)
    qT_d = scratch("qT_d", [NSEG, D, SEG], BF16)
    kTs_d = scratch("kTs_d", [2, D, SEG], BF16)
    vs_d = scratch("vs_d", [2, 8, SEG, 128], BF16)
    gb_d = scratch("gb_d", [NSEG, D, SEG], BF16)
    af_d = scratch("af_d", [NSEG, 512, SEG], BF16)
    cf_d = scratch("cf_d", [NSEG, 512, SEG], BF16)
    og_d = scratch("og_d", [NSEG, D, SEG], BF16)
    ropeC = scratch("ropeC", [128, 2 * SEG], F32)
    ropeS = scratch("ropeS", [128, 2 * SEG], F32)
    kT_loc = [[scratch(f"kT_loc{l}_{h}", [128, SEG], BF16) for h in range(8)] for l in range(L)]
    kT_mid = [[scratch(f"kT_mid{l}_{h}", [4 * 128, SEG], BF16) for h in range(8)] for l in range(L)]
    kT_all = [[scratch(f"kT_all{l}_{h}", [8 * 128, SEG], BF16) for h in range(8)] for l in range(L)]
    v_loc = [[scratch(f"v_loc{l}_{h}", [SEG, 128], BF16) for h in range(8)] for l in range(L)]
    v_mid = [[scratch(f"v_mid{l}_{h}", [4 * SEG, 128], BF16) for h in range(8)] for l in range(L)]
    v_all = [[scratch(f"v_all{l}_{h}", [8 * SEG, 128], BF16) for h in range(8)] for l in range(L)]
    hl_loc = [scratch(f"hl_loc{l}", [512, 32], BF16) for l in range(L)]
    hl_mid = [scratch(f"hl_mid{l}", [4 * 512, 32], BF16) for l in range(L)]
    hl_all = [scratch(f"hl_all{l}", [8 * 512, 32], BF16) for l in range(L)]
    G4 = [[0, 1, 2, 3], [4, 5, 6, 7]]
    G2 = [[0, 4], [1, 5], [2, 6], [3, 7]]

    B_x = {"x0": [Buf(f"x0_{s}") for s in range(NSEG)], "x1": [Buf(f"x1_{s}") for s in range(NSEG)],
           "x2": [Buf(f"x2_{s}") for s in range(NSEG)]}
    B_hT = bufs("hTd", NSEG)
    B_qT = bufs("qTd", NSEG)
    B_kTs = bufs("kTsd", 2)
    B_vs = [bufs(f"vsd{i}_", 8) for i in range(2)]
    B_gb = bufs("gbd", NSEG)
    B_af = bufs("afd", NSEG)
    B_cf = bufs("cfd", NSEG)
    B_og = bufs("ogd", NSEG)
    B_rope = Buf("rope")
    B_kloc = [bufs(f"kloc{l}_", 8) for l in range(L)]
    B_kmid = [bufs(f"kmid{l}_", 8) for l in range(L)]
    B_kall = [bufs(f"kall{l}_", 8) for l in range(L)]
    B_vloc = [bufs(f"vloc{l}_", 8) for l in range(L)]
    B_vmid = [bufs(f"vmid{l}_", 8) for l in range(L)]
    B_vall = [bufs(f"vall{l}_", 8) for l in range(L)]
    B_hloc = bufs("hloc", L)
    B_hmid = bufs("hmid", L)
    B_hall = bufs("hall", L)

    with ExitStack() as gst:
        P = Prog(nc, gst)

        uid = [0]

        def sb(stk, name, shape, dt):
            uid[0] += 1
            return stk.enter_context(nc.sbuf_tensor(f"{name}_u{uid[0]}", list(shape), dt))

        def ACT(out, in_, func, reads, writes, **kw):
            return P.op("act", lambda e: e.activation(out=out, in_=in_, func=func, **kw), reads, writes)

        def TT(eng, out, a, b, op, reads, writes):
            return P.op(eng, lambda e: e.tensor_tensor(out=out, in0=a, in1=b, op=op), reads, writes)

        def TS(eng, out, a, s1, s2, op0, op1, reads, writes):
            if s2 is None:
                return P.op(eng, lambda e: e.tensor_scalar(out=out, in0=a, scalar1=s1, scalar2=None, op0=op0),
                            reads, writes)
            return P.op(eng, lambda e: e.tensor_scalar(out=out, in0=a, scalar1=s1, scalar2=s2, op0=op0, op1=op1),
                        reads, writes)

        def STT(out, in0, scalar, in1, op0, op1, reads, writes):
            return P.op("dve", lambda e: e.scalar_tensor_tensor(out=out, in0=in0, scalar=scalar, in1=in1,
                                                                op0=op0, op1=op1), reads, writes)

        def CP(eng, out, in_, reads, writes):
            return P.op(eng, lambda e: e.tensor_copy(out, in_), reads, writes)

        def MSET(eng, ap, val, writes):
            return P.op(eng, lambda e: e.memset(ap, val), (), writes)

        def MM(mms, reads, writes):
            def fn(e, mms=tuple(mms)):
                ins = None
                for (o, lt, rh, st, sp_) in mms:
                    ins = e.matmul(o, lt, rh, start=st, stop=sp_)
                return ins
            return P.op("pe", fn, reads, writes)

        ones_f = sb(gst, "ones_f", [128, 128], F32)
        ones_b = sb(gst, "ones_b", [128, 128], BF16)
        mhalf = sb(gst, "mhalf", [128, BLK], F32)
        prm = sb(gst, "prm_sb", [128, NPRM], F32)
        sgw_f = sb(gst, "sgw_f", [128, L * 4 * 128], F32)
        sgw_b = sb(gst, "sgw_b", [128, L * 4 * 128], BF16)
        lam_sb = sb(gst, "lam_sb", [128, 4], F32)
        ps = gst.enter_context(nc.psum_tensor("ps", [128, 8 * 512], F32))
        BK = bufs("bank", 8)
        B_const = Buf("const")
        B_prm = Buf("prm")

        def bank(i):
            return ps[:, i * 512:(i + 1) * 512]

        def pcol(o, n=1):
            return prm[:, o:o + n]

        MSET("dve", ones_f[:], 1.0, [B_const])
        MSET("dve", ones_b[:], 1.0, [B_const])
        MSET("pool", mhalf[:], -0.5, [B_const])
        P.dma(prm[:], prm_d, writes=[B_prm])
        P.dma(sgw_f[:], sgw_d, writes=[B_prm])
        CP("pool", sgw_b[:], sgw_f[:], [B_prm], [B_const])

        with ExitStack() as st:
            lt = sb(st, "lam_t", [128, 8, 64], F32)
            ls = sb(st, "lam_s", [128, 8], F32)
            B_lt = Buf("lt")
            for l in range(L):
                for m in range(2):
                    o = O_LAMV + (l * 4 + 2 * m) * 64
                    TT("dve", lt[:, l * 2 + m, :], prm[:, o:o + 64], prm[:, o + 64:o + 128], ALU.mult,
                       [B_prm], [B_lt])
                    P.op("dve", lambda e, a=ls[:, l * 2 + m:l * 2 + m + 1], b=lt[:, l * 2 + m, :]:
                         e.reduce_sum(out=a, in_=b, axis=AX.X), [B_lt], [B_lt])
            ACT(ls[:, 4:8], ls[:, 0:4], AF.Exp, [B_lt], [B_lt])
            for l in range(L):
                li = lambda_init_fn(l)
                TT("dve", lam_sb[:, l:l + 1], ls[:, 4 + 2 * l:5 + 2 * l], ls[:, 5 + 2 * l:6 + 2 * l], ALU.subtract,
                   [B_lt], [B_const])
                TS("dve", lam_sb[:, l:l + 1], lam_sb[:, l:l + 1], float(li), None, ALU.add, None, [B_const], [B_const])
                TS("dve", lam_sb[:, 2 + l:3 + l], pcol(O_SUBG + l), float(1.0 - li), None, ALU.mult, None,
                   [B_prm], [B_const])

            posb = sb(st, "posb", [128, BLK], F32)
            ang = sb(st, "ang", [128, BLK], F32)
            uu = sb(st, "uu", [128, BLK], F32)
            qq = sb(st, "qq", [128, BLK], F32)
            qi = sb(st, "qi", [128, BLK], I32)
            so = sb(st, "so", [2, 128, BLK], F32) if False else None
            so0 = sb(st, "so0", [128, BLK], F32)
            so1 = sb(st, "so1", [128, BLK], F32)
            B_pos, B_ang, B_uu, B_qq, B_qi = Buf("pos"), Buf("ang"), Buf("uu"), Buf("qq"), Buf("qi")
            B_so = bufs("so", 2)
            for pb in range(2 * SEG // BLK):
                cs = slice(pb * BLK, (pb + 1) * BLK)
                P.dma(posb[:], pos_d[:, cs], writes=[B_pos])
                TS("dve", ang[:], posb[:], pcol(O_INVF), None, ALU.mult, None, [B_pos, B_prm], [B_ang])
                for which, (shift, sot, dst) in enumerate(((0.0, so0, ropeS), (float(np.pi / 2), so1, ropeC))):
                    TS("dve", uu[:], ang[:], shift, None, ALU.add, None, [B_ang], [B_uu])
                    TS("dve", qq[:], uu[:], 1.0 / TWO_PI, None, ALU.mult, None, [B_uu], [B_qq])
                    CP("dve", qi[:], qq[:], [B_qq], [B_qi])
                    CP("dve", qq[:], qi[:], [B_qi], [B_qq])
                    STT(uu[:], qq[:], -TWO_PI, uu[:], ALU.mult, ALU.add, [B_qq, B_uu], [B_uu])
                    TS("dve", qq[:], uu[:], float(np.pi), None, ALU.is_gt, None, [B_uu], [B_qq])
                    STT(uu[:], qq[:], -TWO_PI, uu[:], ALU.mult, ALU.add, [B_qq, B_uu], [B_uu])
                    TS("dve", qq[:], uu[:], float(-np.pi), None, ALU.is_lt, None, [B_uu], [B_qq])
                    STT(uu[:], qq[:], TWO_PI, uu[:], ALU.mult, ALU.add, [B_qq, B_uu], [B_uu])
                    ACT(sot[:], uu[:], AF.Sin, [B_uu], [B_so[which]])
                    if which == 0:
                        TS("dve", sot[:], sot[:], pcol(O_SGN), None, ALU.mult, None, [B_so[0], B_prm], [B_so[0]])
                    P.dma(dst[:, cs], sot[:], reads=[B_so[which]], writes=[B_rope], q="pool")
            P.barrier()
        if stop_after == "setup":
            return _finish(nc, P, yT, None)

        def rstd_from_bank(bk, bkbuf, n_feat, out_ap, out_buf, tmp_ap, tmp_buf):
            TS("dve", tmp_ap, bk, 1.0 / n_feat, EPS, ALU.mult, ALU.add, [bkbuf], [tmp_buf])
            ACT(tmp_ap, tmp_ap, AF.Ln, [tmp_buf], [tmp_buf])
            ACT(out_ap, tmp_ap, AF.Exp, [tmp_buf], [out_buf], scale=-0.5)

        def norm_pass(stk, src, src_bufs, dst, dst_bufs, gcol, tag, dst_kind):
            pass

        for l in range(L):
            xsrc, xsrc_b = (xT, B_x["x0"]) if l == 0 else (x1T, B_x["x1"])
            xdst, xdst_b = (x1T, B_x["x1"]) if l == 0 else (x2T, B_x["x2"])
            w_in_l = w_in[l].rearrange("(k p) c -> p k c", p=128)
            for s in range(NSEG):
                t0 = s * SEG
                rope0 = 0 if s == 0 else SEG
                with ExitStack() as p1:
                    hT = sb(p1, "hT", [128, KC, SEG], BF16)
                    B_h = [[Buf(f"h{k}_{b}") for b in range(NB)] for k in range(KC)]
                    with ExitStack() as st:
                        xb = [sb(st, f"xb{i}", [128, KC, BLK], F32) for i in range(2)]
                        sq = [sb(st, f"sq{i}", [128, BLK], F32) for i in range(2)]
                        rs = [sb(st, f"rs{i}", [128, BLK], F32) for i in range(2)]
                        rt = [sb(st, f"rt{i}", [128, BLK], F32) for i in range(2)]
                        B_xb, B_sq, B_rs, B_rt = bufs("xb", 2), bufs("sq", 2), bufs("rs", 2), bufs("rt", 2)
                        for b in range(NB):
                            i = b % 2
                            tsl = slice(t0 + b * BLK, t0 + (b + 1) * BLK)
                            P.dma(xb[i][:], xsrc.rearrange("(k p) t -> p k t", p=128)[:, :, tsl],
                                  reads=[xsrc_b[s]], writes=[B_xb[i]])
                            bki = 4 + i
                            for k in range(KC):
                                j = k % 2
                                ACT(sq[j][:], xb[i][:, k, :], AF.Square, [B_xb[i]], [B_sq[j]])
                                MM([(bank(bki), ones_f[:], sq[j][:], k == 0, k == KC - 1)], [B_sq[j], B_const],
                                   [BK[bki]])
                            rstd_from_bank(bank(bki), BK[bki], D, rs[i][:], B_rs[i], rt[i][:], B_rt[i])
                            for k in range(KC):
                                STT(hT[:, k, b * BLK:(b + 1) * BLK], xb[i][:, k, :], pcol(O_GN + l * 8 + k),
                                    rs[i][:], ALU.mult, ALU.mult, [B_xb[i], B_rs[i], B_prm], [B_h[k][b]])
                        P.dma(hT_d[s].rearrange("(k p) t -> p k t", p=128), hT[:], reads=[B_h], writes=[B_hT[s]],
                              q="pool")
                        P.barrier()
                    if stop_after == ("p1a", l, s):
                        return _finish(nc, P, yT, None)

                    wst = [sb(p1, f"wst{i}", [128, KC, 256], F32) for i in range(2)]
                    wbf = [sb(p1, f"wbf{i}", [128, KC, 256], BF16) for i in range(2)]
                    B_wst, B_wbf = bufs("wst", 2), bufs("wbf", 2)
                    a_sb = sb(p1, "a_sb", [128, 4, SEG + 2 * HALO], BF16)
                    sga = sb(p1, "sga", [128, 4, SEG], BF16)
                    B_a = [[Buf(f"a{i}_{b}") for b in range(NB)] for i in range(4)]
                    B_ahalo = Buf("ahalo")
                    B_sga = [[Buf(f"sga{i}_{b}") for b in range(NB)] for i in range(4)]
                    wctr = [0]
                    pctr = [0]
                    pend1, pend2, pend2_ready = [], [], []

                    def flush_cc(final=False):
                        while pend1:
                            a_, b_, g_, rb, wb = pend1.pop(0)
                            P.collective(a_, b_, g_, reads=[rb], writes=[wb])
                            pend2_ready.append(pend2.pop(0))
                        if final:
                            while pend2_ready:
                                a_, b_, g_, rb, wb = pend2_ready.pop(0)
                                P.collective(a_, b_, g_, reads=[rb], writes=[wb])

                    def load_w(chunks, swap=False):
                        flush_cc()
                        i = wctr[0] % 2
                        wctr[0] += 1
                        for j, c in enumerate(chunks):
                            P.dma(wst[i][:, :, j * 128:(j + 1) * 128], w_in_l[:, :, c * 128:(c + 1) * 128],
                                  writes=[B_wst[i]] if j == 0 else [B_wst[i]])
                        n = len(chunks) * 128
                        if not swap:
                            ACT(wbf[i][:, :, 0:n], wst[i][:, :, 0:n], AF.Copy, [B_wst[i]], [B_wbf[i]])
                        else:
                            ACT(wbf[i][:, :, 0:128], wst[i][:, :, 0:128], AF.Copy, [B_wst[i]], [B_wbf[i]])
                            src = wst[i][:, :, 0:128].rearrange("p k (c h j) -> p k c h j", c=2, h=2)
                            dst = wbf[i][:, :, 128:256].rearrange("p k (c h j) -> p k c h j", c=2, h=2)
                            for c2 in range(2):
                                CP("dve", dst[:, :, c2, 0, :], src[:, :, c2, 1, :], [B_wst[i]], [B_wbf[i]])
                                CP("dve", dst[:, :, c2, 1, :], src[:, :, c2, 0, :], [B_wst[i]], [B_wbf[i]])
                        return wbf[i], B_wbf[i]

                    def fm_job(wt, wtb, nout, consumer):
                        for b in range(NB):
                            pr = pctr[0] % 2
                            pctr[0] += 1
                            bks = [2 * pr + o for o in range(nout)]
                            for o in range(nout):
                                MM([(bank(bks[o]), wt[:, k, o * 128:(o + 1) * 128], hT[:, k, b * BLK:(b + 1) * BLK],
                                     k == 0, k == KC - 1) for k in range(KC)],
                                   [wtb] + [B_h[k][b] for k in range(KC)], [BK[bks[o]]])
                            consumer(b, bks)

                    with ExitStack() as sA:
                        tA = [sb(sA, f"tA{i}", [128, BLK], F32) for i in range(2)]
                        B_tA = bufs("tA", 2)
                        tctr = [0]
                        for i4 in range(4):
                            wt, wtb = load_w([CH_AIN + i4, CH_AIN + 4 + i4])

                            def cons(b, bks, i4=i4):
                                ti = tctr[0] % 2
                                tctr[0] += 1
                                ACT(tA[ti][:], bank(bks[1]), AF.Sigmoid, [BK[bks[1]]], [B_tA[ti]])
                                TT("dve", a_sb[:, i4, HALO + b * BLK:HALO + (b + 1) * BLK], bank(bks[0]), tA[ti][:],
                                   ALU.mult, [BK[bks[0]], B_tA[ti]], [B_a[i4][b]])
                            fm_job(wt, wtb, 2, cons)
                        if s == 0:
                            P.dma(hl_loc[l].rearrange("(i p) c -> p i c", p=128)[:, :, 0:HALO],
                                  a_sb[:, :, HALO:2 * HALO], reads=[B_a], writes=[B_hloc[l]], q="pool")
                            P.dma(hl_loc[l].rearrange("(i p) c -> p i c", p=128)[:, :, HALO:2 * HALO],
                                  a_sb[:, :, SEG:SEG + HALO], reads=[B_a], writes=[B_hloc[l]], q="pool")
                            P.collective(hl_loc[l], hl_mid[l], G4, reads=[B_hloc[l]], writes=[B_hmid[l]])
                            P.collective(hl_mid[l], hl_all[l], G2, reads=[B_hmid[l]], writes=[B_hall[l]])
                        else:
                            MSET("pool", a_sb[:, :, 0:HALO], 0.0, [B_ahalo])
                            MSET("pool", a_sb[:, :, SEG + HALO:SEG + 2 * HALO], 0.0, [B_ahalo])
                        for j2 in range(2):
                            wt, wtb = load_w([CH_AGATE + 2 * j2, CH_AGATE + 2 * j2 + 1])

                            def cons(b, bks, j2=j2):
                                for o in range(2):
                                    ACT(sga[:, 2 * j2 + o, b * BLK:(b + 1) * BLK], bank(bks[o]), AF.Silu,
                                        [BK[bks[o]]], [B_sga[2 * j2 + o][b]])
                            fm_job(wt, wtb, 2, cons)
                        P.barrier()
                    if stop_after == ("p1A", l, s):
                        return _finish(nc, P, yT, None)

                    with ExitStack() as sB:
                        cosb = sb(sB, "cosb", [128, SEG], F32)
                        sinb = sb(sB, "sinb", [128, SEG], F32)
                        B_cs = Buf("cs")
                        P.dma(cosb[:], ropeC[:, rope0:rope0 + SEG], reads=[B_rope], writes=[B_cs])
                        P.dma(sinb[:], ropeS[:, rope0:rope0 + SEG], reads=[B_rope], writes=[B_cs])
                        qst = [sb(sB, f"qst{i}", [128, SEG], BF16) for i in range(2)]
                        B_qst = [[Buf(f"qst{i}_{b}") for b in range(NB)] for i in range(2)]
                        t1 = [sb(sB, f"t1_{i}", [128, BLK], F32) for i in range(2)]
                        t2 = [sb(sB, f"t2_{i}", [128, BLK], F32) for i in range(2)]
                        B_t1, B_t2 = bufs("t1", 2), bufs("t2", 2)
                        vst = [sb(sB, f"vst{i}", [128, SEG // 128, 256], BF16) for i in range(2)]
                        B_vst = [[Buf(f"vst{i}_{tt}") for tt in range(SEG // 128)] for i in range(2)]
                        qctr = [0]
                        tctr = [0]
                        for which in range(2):
                            for h in range(8):
                                wt, wtb = load_w([(CH_Q if which == 0 else CH_K) + h], swap=True)
                                qi_ = qctr[0] % 2
                                qctr[0] += 1

                                def cons(b, bks, qi_=qi_):
                                    ti = tctr[0] % 2
                                    tctr[0] += 1
                                    bs = slice(b * BLK, (b + 1) * BLK)
                                    TT("dve", t1[ti][:], bank(bks[0]), cosb[:, bs], ALU.mult, [BK[bks[0]], B_cs],
                                       [B_t1[ti]])
                                    TT("dve", t2[ti][:], bank(bks[1]), sinb[:, bs], ALU.mult, [BK[bks[1]], B_cs],
                                       [B_t2[ti]])
                                    TT("pool", qst[qi_][:, bs], t1[ti][:], t2[ti][:], ALU.add, [B_t1[ti], B_t2[ti]],
                                       [B_qst[qi_][b]])
                                fm_job(wt, wtb, 2, cons)
                                if which == 0:
                                    P.dma(qT_d[s][h * 128:(h + 1) * 128, :], qst[qi_][:], reads=[B_qst[qi_]],
                                          writes=[B_qT[s]], q="pool")
                                elif s == 0:
                                    P.dma(kT_loc[l][h], qst[qi_][:], reads=[B_qst[qi_]], writes=[B_kloc[l][h]], q="pool")
                                    pend1.append((kT_loc[l][h], kT_mid[l][h], G4, B_kloc[l][h], B_kmid[l][h]))
                                    pend2.append((kT_mid[l][h], kT_all[l][h], G2, B_kmid[l][h], B_kall[l][h]))
                                else:
                                    P.dma(kTs_d[s - 1][h * 128:(h + 1) * 128, :], qst[qi_][:], reads=[B_qst[qi_]],
                                          writes=[B_kTs[s - 1]], q="pool")
                        for vj in range(4):
                            wt, wtb = load_w([CH_V + 2 * vj, CH_V + 2 * vj + 1])
                            vi = vj % 2
                            for tt in range(SEG // 128):
                                bk = pctr[0] % 4
                                pctr[0] += 1
                                MM([(ps[:, bk * 512:bk * 512 + 256], hT[:, k, tt * 128:(tt + 1) * 128], wt[:, k, 0:256],
                                     k == 0, k == KC - 1) for k in range(KC)],
                                   [wtb] + [B_h[k][tt // 4] for k in range(KC)], [BK[bk]])
                                if tt % 2 == 0:
                                    ACT(vst[vi][:, tt, :], ps[:, bk * 512:bk * 512 + 256], AF.Copy, [BK[bk]],
                                        [B_vst[vi][tt]])
                                else:
                                    CP("dve", vst[vi][:, tt, :], ps[:, bk * 512:bk * 512 + 256], [BK[bk]],
                                       [B_vst[vi][tt]])
                            for hh in range(2):
                                h = 2 * vj + hh
                                if s == 0:
                                    P.dma(v_loc[l][h].rearrange("(n p) c -> p n c", p=128),
                                          vst[vi][:, :, hh * 128:(hh + 1) * 128], reads=[B_vst[vi]],
                                          writes=[B_vloc[l][h]], q="pool")
                                    pend1.append((v_loc[l][h], v_mid[l][h], G4, B_vloc[l][h], B_vmid[l][h]))
                                    pend2.append((v_mid[l][h], v_all[l][h], G2, B_vmid[l][h], B_vall[l][h]))
                                else:
                                    P.dma(vs_d[s - 1][h].rearrange("(n p) c -> p n c", p=128),
                                          vst[vi][:, :, hh * 128:(hh + 1) * 128], reads=[B_vst[vi]],
                                          writes=[B_vs[s - 1][h]], q="pool")
                        for j4 in range(4):
                            wt, wtb = load_w([CH_BG + 2 * j4, CH_BG + 2 * j4 + 1])
                            qis = []
                            for o in range(2):
                                qis.append(qctr[0] % 2)
                                qctr[0] += 1

                            def cons(b, bks, qis=qis):
                                for o in range(2):
                                    ACT(qst[qis[o]][:, b * BLK:(b + 1) * BLK], bank(bks[o]), AF.Silu, [BK[bks[o]]],
                                        [B_qst[qis[o]][b]])
                            fm_job(wt, wtb, 2, cons)
                            for o in range(2):
                                c = 2 * j4 + o
                                P.dma(gb_d[s][c * 128:(c + 1) * 128, :], qst[qis[o]][:], reads=[B_qst[qis[o]]],
                                      writes=[B_gb[s]], q="pool")
                        P.barrier()
                    if stop_after == ("p1B", l, s):
                        return _finish(nc, P, yT, None)

                    with ExitStack() as sA2:
                        c_sb = sb(sA2, "c_sb", [128, 4, SEG], F32)
                        B_c = [[Buf(f"c{i}_{b}") for b in range(NB)] for i in range(4)]
                        if s == 0:
                            hall = sb(sA2, "hall", [128, 32, 32], BF16)
                            B_hl = Buf("hl")
                            P.dma(hall[:], hl_all[l].rearrange("(r p) c -> p r c", p=128), reads=[B_hall[l]],
                                  writes=[B_hl])
                            MSET("dve", a_sb[:, :, 0:HALO], 0.0, [B_ahalo])
                            MSET("dve", a_sb[:, :, SEG + HALO:SEG + 2 * HALO], 0.0, [B_ahalo])
                            for r in range(8):
                                STT(a_sb[:, :, 0:HALO], hall[:, 4 * r:4 * r + 4, HALO:2 * HALO], pcol(O_SEL + r),
                                    a_sb[:, :, 0:HALO], ALU.mult, ALU.add, [B_hl, B_prm, B_ahalo], [B_ahalo])
                                STT(a_sb[:, :, SEG + HALO:SEG + 2 * HALO], hall[:, 4 * r:4 * r + 4, 0:HALO],
                                    pcol(O_SEL + 8 + r), a_sb[:, :, SEG + HALO:SEG + 2 * HALO], ALU.mult, ALU.add,
                                    [B_hl, B_prm, B_ahalo], [B_ahalo])
                        for i4 in range(4):
                            cwb = O_CW + (l * 4 + i4) * CONV_K
                            TS("dve", c_sb[:, i4, :], a_sb[:, i4, 0:SEG], pcol(cwb), pcol(O_CB + l * 4 + i4),
                               ALU.mult, ALU.add, [B_a[i4], B_ahalo, B_prm], [B_c[i4]])
                            for j in range(1, CONV_K):
                                STT(c_sb[:, i4, :], a_sb[:, i4, j:j + SEG], pcol(cwb + j), c_sb[:, i4, :], ALU.mult,
                                    ALU.add, [B_a[i4], B_ahalo, B_prm, B_c[i4]], [B_c[i4]])
                        sq = [sb(sA2, f"sqA{i}", [128, BLK], F32) for i in range(2)]
                        rs = [sb(sA2, f"rsA{i}", [128, BLK], F32) for i in range(2)]
                        rt = [sb(sA2, f"rtA{i}", [128, BLK], F32) for i in range(2)]
                        an = [sb(sA2, f"anA{i}", [128, BLK], F32) for i in range(2)]
                        sn = [sb(sA2, f"snA{i}", [128, BLK], F32) for i in range(2)]
                        B_sq, B_rs, B_rt, B_an, B_sn = bufs("sqA", 2), bufs("rsA", 2), bufs("rtA", 2), bufs("anA", 2), \
                            bufs("snA", 2)
                        cnt = 0
                        for b in range(NB):
                            bs = slice(b * BLK, (b + 1) * BLK)
                            bki = 4 + b % 2
                            ri = b % 2
                            for i4 in range(4):
                                j = cnt % 2
                                cnt += 1
                                ACT(sq[j][:], c_sb[:, i4, bs], AF.Square, [B_c[i4][b]], [B_sq[j]])
                                MM([(bank(bki), ones_f[:], sq[j][:], i4 == 0, i4 == 3)], [B_sq[j], B_const], [BK[bki]])
                            rstd_from_bank(bank(bki), BK[bki], 512, rs[ri][:], B_rs[ri], rt[ri][:], B_rt[ri])
                            for i4 in range(4):
                                j = cnt % 2
                                cnt += 1
                                STT(an[j][:], c_sb[:, i4, bs], pcol(O_CG + l * 4 + i4), rs[ri][:], ALU.mult, ALU.mult,
                                    [B_c[i4][b], B_rs[ri], B_prm], [B_an[j]])
                                ACT(sn[j][:], an[j][:], AF.Silu, [B_an[j]], [B_sn[j]])
                                TT("pool", a_sb[:, i4, HALO + b * BLK:HALO + (b + 1) * BLK], sn[j][:], sga[:, i4, bs],
                                   ALU.mult, [B_sn[j], B_sga[i4][b]], [B_a[i4][b]])
                        P.dma(af_d[s].rearrange("(i p) t -> p i t", p=128), a_sb[:, :, HALO:HALO + SEG], reads=[B_a],
                              writes=[B_af[s]], q="pool")
                        P.barrier()
                    if stop_after == ("p1A2", l, s):
                        return _finish(nc, P, yT, None)

                    with ExitStack() as sC:
                        cug = sb(sC, "cug", [128, 4, SEG], BF16)
                        B_cug = [[Buf(f"cug{i}_{tt}") for tt in range(SEG // 128)] for i in range(4)]
                        tC = [sb(sC, f"tC{i}", [128, BLK], F32) for i in range(2)]
                        B_tC = bufs("tC", 2)
                        tctr = [0]
                        for i4 in range(4):
                            wt, wtb = load_w([CH_CU + i4, CH_CG + i4])

                            def cons(b, bks, i4=i4):
                                ti = tctr[0] % 2
                                tctr[0] += 1
                                ACT(tC[ti][:], bank(bks[1]), AF.Silu, [BK[bks[1]]], [B_tC[ti]])
                                TT("dve", cug[:, i4, b * BLK:(b + 1) * BLK], bank(bks[0]), tC[ti][:], ALU.mult,
                                   [BK[bks[0]], B_tC[ti]], [B_cug[i4][4 * b:4 * b + 4]])
                            fm_job(wt, wtb, 2, cons)
                        if stop_after == ("p1C1", l, s):
                            return _finish(nc, P, yT, None)
                        sqv = [sb(sC, f"sqv{i}", [128, BLK], F32) for i in range(2)]
                        ssv = [sb(sC, f"ssv{i}", [128, 2], F32) for i in range(2)]
                        cvn = [sb(sC, f"cvn{i}", [128, BLK], BF16) for i in range(2)]
                        tmx = [sb(sC, f"tmx{i}", [128, BLK], F32) for i in range(2)]
                        B_sqv, B_ssv, B_cvn, B_tmx = bufs("sqv", 2), bufs("ssv", 2), bufs("cvn", 2), bufs("tmx", 2)
                        wts = []
                        for j2 in range(2):
                            wts.append(load_w([CH_CV + 2 * j2, CH_CV + 2 * j2 + 1]))
                        import os
                        for tt in range(int(os.environ.get("K_NTT", SEG // 128))):
                            i = tt % 2
                            bk = pctr[0] % 4
                            pctr[0] += 1
                            mms = []
                            for j2 in range(2):
                                wt, wtb = wts[j2]
                                mms += [(ps[:, bk * 512 + j2 * 256:bk * 512 + (j2 + 1) * 256],
                                         hT[:, k, tt * 128:(tt + 1) * 128], wt[:, k, 0:256], k == 0, k == KC - 1)
                                        for k in range(KC)]
                            MM(mms, [wts[0][1], wts[1][1]] + [B_h[k][tt // 4] for k in range(KC)], [BK[bk]])
                            ACT(sqv[i][:], bank(bk), AF.Square, [BK[bk]], [B_sqv[i]])
                            P.op("dve", lambda e, a=ssv[i][:, 0:1], b_=sqv[i][:]: e.reduce_sum(out=a, in_=b_, axis=AX.X),
                                 [B_sqv[i]], [B_ssv[i]])
                            TS("dve", ssv[i][:, 0:1], ssv[i][:, 0:1], 1.0 / 512, EPS, ALU.mult, ALU.add, [B_ssv[i]],
                               [B_ssv[i]])
                            ACT(ssv[i][:, 0:1], ssv[i][:, 0:1], AF.Ln, [B_ssv[i]], [B_ssv[i]])
                            ACT(ssv[i][:, 1:2], ssv[i][:, 0:1], AF.Exp, [B_ssv[i]], [B_ssv[i]], scale=-0.5)
                            STT(cvn[i][:], bank(bk), ssv[i][:, 1:2], prm[:, O_SGNG + l * 512:O_SGNG + (l + 1) * 512],
                                ALU.mult, ALU.mult, [BK[bk], B_ssv[i], B_prm], [B_cvn[i]])
                            mb = 4 + tt % 2
                            MM([(ps[:, mb * 512 + g * 128:mb * 512 + (g + 1) * 128], cvn[i][:, g * 128:(g + 1) * 128],
                                 sgw_b[:, (l * 4 + g) * 128:(l * 4 + g + 1) * 128], True, True) for g in range(4)],
                               [B_cvn[i], B_const], [BK[mb]])
                            TT("dve", tmx[i][:], bank(mb), prm[:, O_SGUB + l * 512:O_SGUB + (l + 1) * 512], ALU.add,
                               [BK[mb], B_prm], [B_tmx[i]])
                            if os.environ.get("K_CSTOP") == "1":
                                continue
                            cview = cug[:, :, tt * 128:(tt + 1) * 128]
                            TT(os.environ.get("K_CENG", "dve"), cview, tmx[i][:].rearrange("p (g q) -> p g q", g=4), cview, ALU.mult,
                               [B_tmx[i]] + [B_cug[g][tt] for g in range(4)], [B_cug[g][tt] for g in range(4)])
                        P.dma(cf_d[s].rearrange("(i p) t -> p i t", p=128), cug[:], reads=[B_cug], writes=[B_cf[s]],
                              q="pool")
                        flush_cc(final=True)
                        P.barrier()
                    P.barrier()
                if stop_after == ("p1", l, s):
                    return _finish(nc, P, yT, None)

            with ExitStack() as p2:
                KT = sb(p2, "KT", [128, 8 * SEG], BF16)
                VT = sb(p2, "VT", [128, 8 * SEG // 128, 128], BF16)
                B_KT, B_VT = bufs("KT", 8), bufs("VT", 8)
                QT = [sb(p2, f"QT{i}", [128, SEG], BF16) for i in range(2)]
                GB = [sb(p2, f"GB{i}", [128, SEG], BF16) for i in range(2)]
                B_QT, B_GB = bufs("QT", 2), bufs("GB", 2)
                NPT, NPS = 6, 4
                PT = [sb(p2, f"PT{i}", [128, 1024], BF16) for i in range(NPT)]
                B_PT = bufs("PT", NPT)
                PQ = [sb(p2, f"PQ{i}", [128, 1024], BF16) for i in range(2)]
                B_PQ = bufs("PQ", 2)
                PS = [sb(p2, f"PS{i}", [128, 1024], BF16) for i in range(NPS)]
                B_PS = bufs("PS", NPS)
                ogst = [sb(p2, f"ogst{i}", [128, SEG], BF16) for i in range(2)]
                B_ogst = [[Buf(f"ogst{i}_{b}") for b in range(NB)] for i in range(2)]
                ep = {nm: [sb(p2, f"ep_{nm}{i}", [128, BLK], F32) for i in range(2)]
                      for nm in ("c1", "c2", "r1", "r2", "o1", "o2", "os", "sq", "rt", "rs", "on")}
                B_ep = {nm: bufs("ep_" + nm, 2) for nm in ep}
                hctr = 0
                pctr2 = 0
                ectr = 0
                gctr = 0
                deferred = []
                DEF_J, SUM_LAG = 12, 11
                for s in range(NSEG):
                    nkc = 8 if s == 0 else 1
                    for h in range(8):
                        hi = hctr % 2
                        hctr += 1
                        if s == 0:
                            kv = kT_all[l][h].rearrange("(r p) t -> p r t", p=128)
                            vv = v_all[l][h].rearrange("(r n p) d -> p r n d", r=8, p=128)
                            for r in range(8):
                                P.dma(KT[:, r * SEG:(r + 1) * SEG], kv[:, r, :], reads=[B_kall[l][h]],
                                      writes=[B_KT[r]])
                            for r in range(8):
                                P.dma(VT[:, r * 16:(r + 1) * 16, :], vv[:, r, :, :], reads=[B_vall[l][h]],
                                      writes=[B_VT[r]])
                        else:
                            P.dma(KT[:, 0:SEG], kTs_d[s - 1][h * 128:(h + 1) * 128, :], reads=[B_kTs[s - 1]],
                                  writes=[B_KT[0]])
                            P.dma(VT[:, 0:16, :], vs_d[s - 1][h].rearrange("(n p) d -> p n d", p=128),
                                  reads=[B_vs[s - 1][h]], writes=[B_VT[0]])
                        P.dma(QT[hi][:], qT_d[s][h * 128:(h + 1) * 128, :], reads=[B_qT[s]], writes=[B_QT[hi]])
                        P.dma(GB[hi][:], gb_d[s][h * 128:(h + 1) * 128, :], reads=[B_gb[s]], writes=[B_GB[hi]])
                        nt = nkc * 16
                        ng = nt // 4
                        for qb in range(NB):
                            qs = slice(qb * BLK, (qb + 1) * BLK)

                            def qk(j):
                                pr = (pctr2 + j) % 2
                                ks = slice(j * 128, (j + 1) * 128)
                                MM([(ps[:, pr * 1024:pr * 1024 + 512], KT[0:64, ks], QT[hi][0:64, qs], True, True),
                                    (ps[:, pr * 1024 + 512:pr * 1024 + 1024], KT[64:128, ks], QT[hi][64:128, qs], True,
                                     True)], [B_KT[j // 16], B_QT[hi]], [BK[2 * pr], BK[2 * pr + 1]])
                            qk(0)
                            qk(1)
                            pend = []
                            for j in range(nt):
                                pr = (pctr2 + j) % 2
                                pi = (pctr2 + j) % NPT
                                pim = (pctr2 + j - 1) % NPT
                                ACT(PT[pi][:], ps[:, pr * 1024:(pr + 1) * 1024], AF.Exp, [BK[2 * pr], BK[2 * pr + 1]],
                                    [B_PT[pi]], scale=0.125)
                                if j + 2 < nt:
                                    qk(j + 2)
                                st_, sp_ = (j == 0), (j == nt - 1)
                                MM([(bank(4), VT[:, j, :], PT[pi][:, 0:512], st_, sp_),
                                    (bank(5), VT[:, j, :], PT[pi][:, 512:1024], st_, sp_)],
                                   [B_VT[j // 16], B_PT[pi]], [BK[4], BK[5]])
                                if j % 4 == 1:
                                    TT("dve", PQ[0][:], PT[pim][:], PT[pi][:], ALU.add, [B_PT[pim], B_PT[pi]], [B_PQ[0]])
                                if j % 4 == 3:
                                    g = j // 4
                                    gi = (gctr + g) % NPS
                                    TT("dve", PQ[1][:], PT[pim][:], PT[pi][:], ALU.add, [B_PT[pim], B_PT[pi]], [B_PQ[1]])
                                    TT("dve", PS[gi][:], PQ[0][:], PQ[1][:], ALU.add, [B_PQ[0], B_PQ[1]], [B_PS[gi]])
                                    pend.append(g)
                                while deferred and (j >= deferred[0][0] or j == nt - 1):
                                    deferred.pop(0)[1]()
                                while pend and (j >= 4 * pend[0] + 3 + SUM_LAG or j == nt - 1):
                                    g = pend.pop(0)
                                    gi = (gctr + g) % NPS
                                    MM([(bank(6), ones_b[:], PS[gi][:, 0:512], g == 0, g == ng - 1),
                                        (bank(7), ones_b[:], PS[gi][:, 512:1024], g == 0, g == ng - 1)],
                                       [B_PS[gi], B_const], [BK[6], BK[7]])
                            pctr2 += nt
                            gctr += ng
                            ei = ectr % 2
                            ectr += 1
                            E = {nm: ep[nm][ei] for nm in ep}
                            BE = {nm: B_ep[nm][ei] for nm in ep}
                            CP("dve", E["c1"][:], bank(4), [BK[4]], [BE["c1"]])
                            CP("dve", E["c2"][:], bank(5), [BK[5]], [BE["c2"]])
                            P.op("dve", lambda e, o=E["r2"][:], i_=bank(7): e.reciprocal(out=o, in_=i_), [BK[7]],
                                 [BE["r2"]])
                            STT(E["r1"][:], bank(6), lam_sb[:, l:l + 1], E["r2"][:], ALU.mult, ALU.mult,
                                [BK[6], BE["r2"], B_const], [BE["r1"]])

                            def d2(E=E, BE=BE):
                                ACT(E["o1"][:], bank(6), AF.Square, [BK[6]], [BE["o1"]], scale=1e-3)
                                import os as _os
                                eng_ = _os.environ.get("K_D2ENG", "dve")
                                TT(eng_, E["o2"][:], E["r1"][:], E["c2"][:], ALU.mult, [BE["r1"], BE["c2"]], [BE["o2"]])
                                TT(eng_, E["os"][:], E["c1"][:], E["o2"][:], ALU.subtract, [BE["c1"], BE["o2"]],
                                   [BE["os"]])

                            def d6(E=E, BE=BE):
                                ACT(E["sq"][:], E["os"][:], AF.Square, [BE["os"]], [BE["sq"]])

                            def d9(E=E, BE=BE):
                                MM([(bank(6), ones_f[:], E["sq"][:], True, True)], [BE["sq"], B_const], [BK[6]])
                                STT(E["rt"][:], bank(6), 1.0 / 128, E["o1"][:], ALU.mult, ALU.add,
                                    [BK[6], BE["o1"]], [BE["rt"]])

                            def d12(E=E, BE=BE, hi=hi, qs=qs, qb=qb, h=h, s=s):
                                ACT(E["rt"][:], E["rt"][:], AF.Ln, [BE["rt"]], [BE["rt"]])
                                ACT(E["rs"][:], E["rt"][:], AF.Exp, [BE["rt"]], [BE["rs"]], scale=-0.5)
                                STT(E["on"][:], E["os"][:], lam_sb[:, 2 + l:3 + l], E["rs"][:], ALU.mult, ALU.mult,
                                    [BE["os"], BE["rs"], B_const], [BE["on"]])
                                TT("pool", ogst[hi][:, qs], E["on"][:], GB[hi][:, qs], ALU.mult, [BE["on"], B_GB[hi]],
                                   [B_ogst[hi][qb]])
                                if qb == NB - 1:
                                    P.dma(og_d[s][h * 128:(h + 1) * 128, :], ogst[hi][:], reads=[B_ogst[hi]],
                                          writes=[B_og[s]], q="pool")
                            deferred.append((2, d2))
                            deferred.append((6, d6))
                            deferred.append((9, d9))
                            deferred.append((12, d12))
                while deferred:
                    deferred.pop(0)[1]()
                P.barrier()
            if stop_after == ("p2", l):
                return _finish(nc, P, yT, None)

            with ExitStack() as p3:
                HS = 1024
                h3 = sb(p3, "h3", [128, KC, HS], BF16)
                af3 = sb(p3, "af3", [128, 4, HS], BF16)
                cf3 = sb(p3, "cf3", [128, 4, HS], BF16)
                og3 = sb(p3, "og3", [128, KC, HS], BF16)
                B_in3 = bufs("in3_", 4)
                w3st = [sb(p3, f"w3st{i}", [128, 40, 128], F32) for i in range(2)]
                w3bf = [sb(p3, f"w3bf{i}", [128, 40, 128], BF16) for i in range(2)]
                B_w3st, B_w3bf = bufs("w3st", 2), bufs("w3bf", 2)
                y3 = sb(p3, "y3", [128, KC, HS], BF16)
                B_y3 = [[Buf(f"y3_{j}_{tb}") for tb in range(2)] for j in range(KC)]
                wost = [sb(p3, f"wost{i}", [128, KC, 128], F32) for i in range(2)]
                wobf = [sb(p3, f"wobf{i}", [128, KC, 128], BF16) for i in range(2)]
                B_wost, B_wobf = bufs("wost", 2), bufs("wobf", 2)
                mg = [sb(p3, f"mg{i}", [128, BLK], F32) for i in range(2)]
                ya = [sb(p3, f"ya{i}", [128, BLK], F32) for i in range(2)]
                yt = [sb(p3, f"yt{i}", [128, BLK], F32) for i in range(2)]
                xr = [sb(p3, f"xr{i}", [128, BLK], F32) for i in range(2)]
                xo = [sb(p3, f"xo{i}", [128, BLK], F32) for i in range(2)]
                B_mg, B_ya, B_yt, B_xr, B_xo = bufs("mg", 2), bufs("ya", 2), bufs("yt", 2), bufs("xr", 2), bufs("xo", 2)
                wa_l = w_pa[l].rearrange("(k p) c -> p k c", p=128)
                wb_l = w_pb[l].rearrange("(k p) c -> p k c", p=128)
                wc_l = w_pc[l].rearrange("(k p) c -> p k c", p=128)
                wo_l = w_out[l].rearrange("(k p) c -> p k c", p=128)
                w3c = 0
                g3 = 0
                x3 = 0
                for hs in range(T // HS):
                    s = hs // 2
                    c0 = (hs % 2) * HS
                    P.dma(h3[:], hT_d[s].rearrange("(k p) t -> p k t", p=128)[:, :, c0:c0 + HS], reads=[B_hT[s]],
                          writes=[B_in3[0]])
                    P.dma(af3[:], af_d[s].rearrange("(k p) t -> p k t", p=128)[:, :, c0:c0 + HS], reads=[B_af[s]],
                          writes=[B_in3[1]])
                    P.dma(cf3[:], cf_d[s].rearrange("(k p) t -> p k t", p=128)[:, :, c0:c0 + HS], reads=[B_cf[s]],
                          writes=[B_in3[2]])
                    P.dma(og3[:], og_d[s].rearrange("(k p) t -> p k t", p=128)[:, :, c0:c0 + HS], reads=[B_og[s]],
                          writes=[B_in3[3]])
                    def load3(j):
                        nonlocal w3c
                        wi = w3c % 2
                        w3c += 1
                        js = slice(j * 128, (j + 1) * 128)
                        P.dma(w3st[wi][:, 0:4, :], wa_l[:, :, js], writes=[B_w3st[wi]])
                        P.dma(w3st[wi][:, 4:12, :], wb_l[:, :, js], writes=[B_w3st[wi]])
                        P.dma(w3st[wi][:, 12:16, :], wc_l[:, :, js], writes=[B_w3st[wi]])
                        for m in range(3):
                            mc = CH_M * 128 + m * D + j * 128
                            P.dma(w3st[wi][:, 16 + 8 * m:24 + 8 * m, :], w_in_l[:, :, mc:mc + 128],
                                  writes=[B_w3st[wi]])
                        ACT(w3bf[wi][:], w3st[wi][:], AF.Copy, [B_w3st[wi]], [B_w3bf[wi]])
                        return wi

                    nxt3 = load3(0)
                    for j in range(KC):
                        wi = nxt3
                        if j + 1 < KC:
                            nxt3 = load3(j + 1)
                        js = slice(j * 128, (j + 1) * 128)
                        W = w3bf[wi]
                        for tb in range(2):
                            ts_ = slice(tb * BLK, (tb + 1) * BLK)
                            yi = (j * 2 + tb) % 2
                            branches = ((af3, B_in3[1], 0, 4), (og3, B_in3[3], 4, 8), (cf3, B_in3[2], 12, 4))
                            for m, (src, srcb, wo_, nk) in enumerate(branches):
                                gi = g3 % 2
                                g3 += 1
                                gbk, ybk = gi, 2 + gi
                                MM([(bank(gbk), W[:, 16 + 8 * m + k, :], h3[:, k, ts_], k == 0, k == KC - 1)
                                    for k in range(KC)], [B_w3bf[wi], B_in3[0]], [BK[gbk]])
                                MM([(bank(ybk), W[:, wo_ + k, :], src[:, k, ts_], k == 0, k == nk - 1)
                                    for k in range(nk)], [B_w3bf[wi], srcb], [BK[ybk]])
                                ACT(mg[gi][:], bank(gbk), AF.Sigmoid, [BK[gbk]], [B_mg[gi]])
                                if m == 0:
                                    TT("dve", ya[yi][:], bank(ybk), mg[gi][:], ALU.mult, [BK[ybk], B_mg[gi]],
                                       [B_ya[yi]])
                                else:
                                    TT("dve", yt[gi][:], bank(ybk), mg[gi][:], ALU.mult, [BK[ybk], B_mg[gi]],
                                       [B_yt[gi]])
                                    if m == 1:
                                        TT("pool", ya[yi][:], ya[yi][:], yt[gi][:], ALU.add, [B_ya[yi], B_yt[gi]],
                                           [B_ya[yi]])
                                    else:
                                        TT("pool", y3[:, j, ts_], ya[yi][:], yt[gi][:], ALU.add,
                                           [B_ya[yi], B_yt[gi]], [B_y3[j][tb]])
                    for j in range(KC):
                        wi = w3c % 2
                        w3c += 1
                        js = slice(j * 128, (j + 1) * 128)
                        P.dma(wost[wi][:], wo_l[:, :, js], writes=[B_wost[wi]])
                        CP("dve", wobf[wi][:], wost[wi][:], [B_wost[wi]], [B_wobf[wi]])
                        for tb in range(2):
                            ts_ = slice(tb * BLK, (tb + 1) * BLK)
                            tg = slice(hs * HS + tb * BLK, hs * HS + (tb + 1) * BLK)
                            xi = x3 % 2
                            x3 += 1
                            bk = 4 + xi
                            P.dma(xr[xi][:], xsrc[js, tg], reads=[xsrc_b[s]], writes=[B_xr[xi]])
                            MM([(bank(bk), wobf[wi][:, k, :], y3[:, k, ts_], k == 0, k == KC - 1) for k in range(KC)],
                               [B_wobf[wi]] + [B_y3[k][tb] for k in range(KC)], [BK[bk]])
                            TT("dve", xo[xi][:], bank(bk), xr[xi][:], ALU.add, [BK[bk], B_xr[xi]], [B_xo[xi]])
                            P.dma(xdst[js, tg], xo[xi][:], reads=[B_xo[xi]], writes=[xdst_b[s]], q="pool")
                P.barrier()
            if stop_after == ("p3", l):
                return _finish(nc, P, yT, None)

        with ExitStack() as st:
            xb = [sb(st, f"fxb{i}", [128, KC, BLK], F32) for i in range(2)]
            yo = [sb(st, f"fyo{i}", [128, KC, BLK], F32) for i in range(2)]
            sq = [sb(st, f"fsq{i}", [128, BLK], F32) for i in range(2)]
            rs = [sb(st, f"frs{i}", [128, BLK], F32) for i in range(2)]
            rt = [sb(st, f"frt{i}", [128, BLK], F32) for i in range(2)]
            B_xb, B_yo, B_sq, B_rs, B_rt = bufs("fxb", 2), bufs("fyo", 2), bufs("fsq", 2), bufs("frs", 2), bufs("frt", 2)
            B_y = Buf("y")
            for b in range(T // BLK):
                i = b % 2
                s = b // NB
                tsl = slice(b * BLK, (b + 1) * BLK)
                P.dma(xb[i][:], x2T.rearrange("(k p) t -> p k t", p=128)[:, :, tsl], reads=[B_x["x2"][s]],
                      writes=[B_xb[i]])
                bki = 4 + i
                for k in range(KC):
                    j = k % 2
                    ACT(sq[j][:], xb[i][:, k, :], AF.Square, [B_xb[i]], [B_sq[j]])
                    MM([(bank(bki), ones_f[:], sq[j][:], k == 0, k == KC - 1)], [B_sq[j], B_const], [BK[bki]])
                rstd_from_bank(bank(bki), BK[bki], D, rs[i][:], B_rs[i], rt[i][:], B_rt[i])
                for k in range(KC):
                    STT(yo[i][:, k, :], xb[i][:, k, :], pcol(O_FG + k), rs[i][:], ALU.mult, ALU.mult,
                        [B_xb[i], B_rs[i], B_prm], [B_yo[i]])
                P.dma(yT.rearrange("(k p) t -> p k t", p=128)[:, :, tsl], yo[i][:], reads=[B_yo[i]], writes=[B_y],
                      q="pool")
        return _finish(nc, P, yT, None)


def _finish(nc, P, yT, _):
    P.barrier()
    with nc.Block() as block:
        P.emit(block)
    return nc


def _pack_params(core, norm_g, final_g, conv_w, conv_b, conv_norm_g, subln_g, lam_q1, lam_k1, lam_q2, lam_k2,
                 sgu_norm_g, sgu_b):
    prm = np.zeros((128, NPRM), np.float32)
    p = np.arange(128)
    for l in range(L):
        prm[:, O_GN + l * 8:O_GN + (l + 1) * 8] = norm_g[l].reshape(8, 128).T
        for i in range(4):
            prm[:, O_CW + (l * 4 + i) * CONV_K:O_CW + (l * 4 + i + 1) * CONV_K] = conv_w[l][:, i * 128:(i + 1) * 128].T
        prm[:, O_CB + l * 4:O_CB + (l + 1) * 4] = conv_b[l].reshape(4, 128).T
        prm[:, O_CG + l * 4:O_CG + (l + 1) * 4] = conv_norm_g[l].reshape(4, 128).T
        prm[:, O_SUBG + l] = subln_g[l]
        for v, arr in enumerate((lam_q1, lam_k1, lam_q2, lam_k2)):
            prm[:, O_LAMV + (l * 4 + v) * 64:O_LAMV + (l * 4 + v + 1) * 64] = arr[l][None, :]
        prm[:, O_SGNG + l * 512:O_SGNG + (l + 1) * 512] = sgu_norm_g[l][None, :]
        prm[:, O_SGUB + l * 512:O_SGUB + (l + 1) * 512] = sgu_b[l].reshape(1, 512)
    prm[:, O_FG:O_FG + 8] = final_g.reshape(8, 128).T
    half = 32
    inv = (np.float32(10000.0) ** (-(np.arange(half, dtype=np.float32) / np.float32(half)))).astype(np.float32)
    d = p % 64
    prm[:, O_INVF] = inv[d % 32]
    prm[:, O_SGN] = np.where(d < 32, -1.0, 1.0)
    if core > 0:
        prm[:, O_SEL + core - 1] = 1.0
    if core < NCORES - 1:
        prm[:, O_SEL + 8 + core + 1] = 1.0
    return prm


_NC_CACHE = {}


def make_in_maps(x_prompt, x_sample, norm_g, w_in, conv_w, conv_b, conv_norm_g, w_proj_a,
                 lam_q1, lam_k1, lam_q2, lam_k2, subln_g, w_proj_b,
                 sgu_norm_g, sgu_w, sgu_b, w_proj_c, w_out, final_g):
    f = lambda a: np.ascontiguousarray(np.asarray(a, dtype=np.float32))
    x_prompt, x_sample = f(x_prompt), f(x_sample)
    w_in, w_proj_a, w_proj_b, w_proj_c, w_out = f(w_in), f(w_proj_a), f(w_proj_b), f(w_proj_c), f(w_out)
    norm_g, final_g, conv_w, conv_b, conv_norm_g = f(norm_g), f(final_g), f(conv_w), f(conv_b), f(conv_norm_g)
    subln_g, sgu_norm_g, sgu_w, sgu_b = f(subln_g), f(sgu_norm_g), f(sgu_w), f(sgu_b)
    lam_q1, lam_k1, lam_q2, lam_k2 = f(lam_q1), f(lam_k1), f(lam_q2), f(lam_k2)
    sgw = np.ascontiguousarray(sgu_w.transpose(3, 0, 1, 2).reshape(128, L * 4 * 128))
    in_maps = []
    for c in range(NCORES):
        xs = np.concatenate([x_prompt[0, c * SEG:(c + 1) * SEG], x_sample[2 * c], x_sample[2 * c + 1]], axis=0)
        pos = np.concatenate([np.arange(c * SEG, (c + 1) * SEG), np.arange(SEG)]).astype(np.float32)
        in_maps.append({
            "xT": np.ascontiguousarray(xs.T),
            "w_in": w_in, "w_pa": w_proj_a, "w_pb": w_proj_b, "w_pc": w_proj_c, "w_out": w_out,
            "prm": _pack_params(c, norm_g, final_g, conv_w, conv_b, conv_norm_g, subln_g, lam_q1, lam_k1, lam_q2,
                                lam_k2, sgu_norm_g, sgu_b),
            "sgw": sgw,
            "pos": np.ascontiguousarray(np.broadcast_to(pos[None, :], (128, 2 * SEG))),
        })
    return in_maps


def kernel(x_prompt, x_sample, norm_g, w_in, conv_w, conv_b, conv_norm_g, w_proj_a,
           lam_q1, lam_k1, lam_q2, lam_k2, subln_g, w_proj_b,
           sgu_norm_g, sgu_w, sgu_b, w_proj_c, w_out, final_g):
    in_maps = make_in_maps(x_prompt, x_sample, norm_g, w_in, conv_w, conv_b, conv_norm_g, w_proj_a,
                           lam_q1, lam_k1, lam_q2, lam_k2, subln_g, w_proj_b,
                           sgu_norm_g, sgu_w, sgu_b, w_proj_c, w_out, final_g)
    if "nc" not in _NC_CACHE:
        _NC_CACHE["nc"] = build_nc()
    res = run_bass_kernel_spmd(_NC_CACHE["nc"], in_maps, core_ids=list(range(NCORES)))
    y_prompt = np.empty((1, NCORES * SEG, D), np.float32)
    y_sample = np.empty((2 * NCORES, SEG, D), np.float32)
    for c in range(NCORES):
        y = np.asarray(res.results[c]["yT"], dtype=np.float32).T
        y_prompt[0, c * SEG:(c + 1) * SEG] = y[0:SEG]
        y_sample[2 * c] = y[SEG:2 * SEG]
        y_sample[2 * c + 1] = y[2 * SEG:3 * SEG]
    return (y_prompt, y_sample)
```
